# Optimizing a Trainium2 kernel written in Bass

```python
import jax, jax.numpy as jnp
from jax import lax
import numpy as np

D_MODEL = 2048
BATCH = 1
SEQ = 16384
DEPTH = 1

N_HEADS = 16
QK_NOPE_DIM = 128
QK_ROPE_DIM = 64
QK_HEAD_DIM = QK_NOPE_DIM + QK_ROPE_DIM
V_HEAD_DIM = 128
Q_LORA_RANK = 512
KV_LORA_RANK = 512
MLA_WIDTH = N_HEADS * V_HEAD_DIM
ROPE_THETA = 10000.0
Q_BLOCK = 128
POOL_WINDOWS = (2, 4, 8, 16)
POOL_GROUPS = len(POOL_WINDOWS)
POOL_GROUP_DIM = 256
POOL_WIDTH = POOL_GROUPS * POOL_GROUP_DIM
N_BRANCHES = 2
RMS_EPS = 1e-6
LN_EPS = 1e-5
DEEPNORM_ALPHA = (2.0 * DEPTH) ** 0.25
DEEPNORM_BETA = (8.0 * DEPTH) ** -0.25

IN_SPLITS = (Q_LORA_RANK, KV_LORA_RANK + QK_ROPE_DIM, MLA_WIDTH, POOL_WIDTH, POOL_WIDTH, N_BRANCHES * D_MODEL)
IN_WIDTH = sum(IN_SPLITS)
IN_OFFSETS = tuple(sum(IN_SPLITS[:i + 1]) for i in range(len(IN_SPLITS) - 1))

kernel_name = 'hybrid_mla_pool_gated_deepnorm_adaln'


def layer_norm(x, g=None, b=None):
    xf = x.astype(jnp.float32)
    mu = jnp.mean(xf, axis=-1, keepdims=True)
    var = jnp.mean(jnp.square(xf - mu), axis=-1, keepdims=True)
    y = (xf - mu) * lax.rsqrt(var + LN_EPS)
    if g is not None:
        y = y * g.astype(jnp.float32) + b.astype(jnp.float32)
    return y.astype(x.dtype)


def rms_norm(x, g):
    xf = x.astype(jnp.float32)
    y = xf * lax.rsqrt(jnp.mean(jnp.square(xf), axis=-1, keepdims=True) + RMS_EPS)
    return (y * g.astype(jnp.float32)).astype(x.dtype)


def rope_tables(positions, dtype):
    inv_freq = ROPE_THETA ** (-jnp.arange(0, QK_ROPE_DIM, 2, dtype=jnp.float32) / QK_ROPE_DIM)
    ang = positions.astype(jnp.float32)[..., None] * inv_freq
    return jnp.cos(ang)[:, :, None, :].astype(dtype), jnp.sin(ang)[:, :, None, :].astype(dtype)


def apply_rope(x, cos, sin):
    half = QK_ROPE_DIM // 2
    x1, x2 = x[..., :half], x[..., half:]
    return jnp.concatenate([x1 * cos - x2 * sin, x1 * sin + x2 * cos], axis=-1)


def causal_attention(q, k, v):
    B, S, H, Dqk = q.shape
    Dv = v.shape[-1]
    nb = S // Q_BLOCK
    scale = Dqk ** -0.5
    q_blocks = q.reshape(B, nb, Q_BLOCK, H, Dqk).transpose(1, 0, 2, 3, 4)
    starts = jnp.arange(nb, dtype=jnp.int32) * Q_BLOCK
    k_idx = jnp.arange(S, dtype=jnp.int32)

    def one_block(args):
        q_blk, start = args
        s = jnp.einsum('bqhd,bkhd->bhqk', q_blk, k, preferred_element_type=jnp.float32) * scale
        q_idx = start + jnp.arange(Q_BLOCK, dtype=jnp.int32)
        causal = k_idx[None, :] <= q_idx[:, None]
        s = jnp.where(causal[None, None], s, -jnp.inf)
        p = jax.nn.softmax(s, axis=-1).astype(v.dtype)
        return jnp.einsum('bhqk,bkhd->bqhd', p, v)

    out = lax.map(one_block, (q_blocks, starts))
    return out.transpose(1, 0, 2, 3, 4).reshape(B, S, H * Dv)


def mla_mixer(q_lat, kv_lat, positions, q_norm_g, w_q_b, kv_norm_g, w_kv_b):
    B, S, _ = q_lat.shape
    q = jnp.einsum('bsr,re->bse', rms_norm(q_lat, q_norm_g), w_q_b).reshape(B, S, N_HEADS, QK_HEAD_DIM)
    q_nope, q_pe = q[..., :QK_NOPE_DIM], q[..., QK_NOPE_DIM:]
    c_kv, k_pe = kv_lat[..., :KV_LORA_RANK], kv_lat[..., KV_LORA_RANK:]
    kv = jnp.einsum('bsr,re->bse', rms_norm(c_kv, kv_norm_g), w_kv_b).reshape(B, S, N_HEADS, QK_NOPE_DIM + V_HEAD_DIM)
    k_nope, v = kv[..., :QK_NOPE_DIM], kv[..., QK_NOPE_DIM:]
    cos, sin = rope_tables(positions, q.dtype)
    q_pe = apply_rope(q_pe, cos, sin)
    k_pe = apply_rope(k_pe[:, :, None, :], cos, sin)
    q_full = jnp.concatenate([q_nope, q_pe], axis=-1)
    k_full = jnp.concatenate([k_nope, jnp.broadcast_to(k_pe, (B, S, N_HEADS, QK_ROPE_DIM))], axis=-1)
    return causal_attention(q_full, k_full, v)


def pool_mixer(u, w_pool_g, pool_scale):
    B, S, _ = u.shape
    uf = u.astype(jnp.float32)
    cs = jnp.concatenate([jnp.zeros((B, 1, POOL_WIDTH), jnp.float32), lax.cumsum(uf, axis=1)], axis=1)
    t = jnp.arange(S, dtype=jnp.int32)
    pooled = []
    for g, w in enumerate(POOL_WINDOWS):
        sl = slice(g * POOL_GROUP_DIM, (g + 1) * POOL_GROUP_DIM)
        lo = jnp.maximum(t + 1 - w, 0)
        win_sum = cs[:, 1:, sl] - jnp.take(cs[:, :, sl], lo, axis=1)
        count = jnp.minimum(t + 1, w).astype(jnp.float32)[None, :, None]
        pooled.append(win_sum / count - uf[..., sl])
    pooled = jnp.stack(pooled, axis=2).astype(u.dtype)
    mixed = jnp.einsum('bsgi,gio->bsgo', pooled, w_pool_g).reshape(B, S, POOL_WIDTH)
    return mixed * pool_scale


def setup_inputs(seed: int = 0) -> dict:
    key = jax.random.key(seed)
    ks = jax.random.split(key, 18)
    L, D = DEPTH, D_MODEL

    def nrm(k, shape, scale):
        return jax.random.normal(k, shape, jnp.float32) * scale

    return {
        'x': nrm(ks[0], (BATCH, SEQ, D), 1.0),
        'c': nrm(ks[1], (BATCH, D), 1.0),
        'positions': jnp.broadcast_to(jnp.arange(SEQ, dtype=jnp.int32), (BATCH, SEQ)),
        'w_ada': nrm(ks[2], (L, D, 3 * D), 0.1 * D ** -0.5),
        'b_ada': nrm(ks[3], (L, 3 * D), 0.01),
        'w_in': nrm(ks[4], (L, D, IN_WIDTH), D ** -0.5),
        'b_gates': nrm(ks[5], (L, N_BRANCHES * D), 0.01),
        'q_norm_g': 1.0 + nrm(ks[6], (L, Q_LORA_RANK), 0.01),
        'w_q_b': nrm(ks[7], (L, Q_LORA_RANK, N_HEADS * QK_HEAD_DIM), Q_LORA_RANK ** -0.5),
        'kv_norm_g': 1.0 + nrm(ks[8], (L, KV_LORA_RANK), 0.01),
        'w_kv_b': nrm(ks[9], (L, KV_LORA_RANK, N_HEADS * (QK_NOPE_DIM + V_HEAD_DIM)), KV_LORA_RANK ** -0.5),
        'w_mla_o': nrm(ks[10], (L, MLA_WIDTH, D), DEEPNORM_BETA * MLA_WIDTH ** -0.5),
        'w_pool_g': nrm(ks[11], (L, POOL_GROUPS, POOL_GROUP_DIM, POOL_GROUP_DIM), POOL_GROUP_DIM ** -0.5),
        'pool_scale': 1.0 + nrm(ks[12], (L, POOL_WIDTH), 0.02),
        'w_pool_o': nrm(ks[13], (L, POOL_WIDTH, D), DEEPNORM_BETA * POOL_WIDTH ** -0.5),
        'w_out': nrm(ks[14], (L, D, D), DEEPNORM_BETA * D ** -0.5),
        'ln_g': 1.0 + nrm(ks[15], (L, D), 0.01),
        'ln_b': nrm(ks[16], (L, D), 0.01),
    }


def reference(x, c, positions, w_ada, b_ada, w_in, b_gates, q_norm_g, w_q_b, kv_norm_g, w_kv_b,
              w_mla_o, w_pool_g, pool_scale, w_pool_o, w_out, ln_g, ln_b):
    for l in range(DEPTH):
        mod = jnp.einsum('bd,de->be', jax.nn.silu(c), w_ada[l]) + b_ada[l]
        shift, scale, gate = jnp.split(mod[:, None, :], 3, axis=-1)
        h = layer_norm(x) * (1.0 + scale) + shift
        proj = jnp.einsum('bsd,de->bse', h, w_in[l])
        q_lat, kv_lat, mla_gate, pool_in, pool_gate, gate_logits = jnp.split(proj, IN_OFFSETS, axis=-1)
        y_mla = mla_mixer(q_lat, kv_lat, positions, q_norm_g[l], w_q_b[l], kv_norm_g[l], w_kv_b[l])
        y_mla = jnp.einsum('bse,ed->bsd', y_mla * jax.nn.silu(mla_gate), w_mla_o[l])
        y_pool = pool_mixer(pool_in, w_pool_g[l], pool_scale[l])
        y_pool = jnp.einsum('bse,ed->bsd', y_pool * jax.nn.silu(pool_gate), w_pool_o[l])
        g_mla, g_pool = jnp.split(jax.nn.sigmoid(gate_logits + b_gates[l]), N_BRANCHES, axis=-1)
        y = jnp.einsum('bsd,de->bse', g_mla * y_mla + g_pool * y_pool, w_out[l])
        x = layer_norm(DEEPNORM_ALPHA * x + (1.0 + gate) * y, ln_g[l], ln_b[l])
    return x
```

```python
import math
from contextlib import ExitStack

import ml_dtypes
import numpy as np

import concourse.bass as bass
import concourse.mybir as mybir
from concourse.bass_utils import run_bass_kernel_spmd

F32 = mybir.dt.float32
BF16 = mybir.dt.bfloat16
I32 = mybir.dt.int32
ALU = mybir.AluOpType
AF = mybir.ActivationFunctionType

NCORES = 8
S = 16384
D = 2048
TOK = S // NCORES
HALO = 128
NT = TOK + HALO
INW = 9280
C_QLAT, C_CKV, C_KPE, C_MG, C_PI, C_PG, C_GL = 0, 512, 1024, 1088, 3136, 4160, 5184
LATR = 1088
SCALE = 192.0 ** -0.5
ALPHA = 2.0 ** 0.25
NQT = S // 512
NGRP = 16
GT = S // NGRP
N_ASEM = 28


class Rec:
    ENGS = ("pe", "act", "dve", "pool", "sp")

    def __init__(self):
        self.ops = []
        self.last_w = {}
        self.readers = {}
        self.last_on = {}
        self.async_since = []
        self.asem_cnt = [0] * N_ASEM
        self.asem_rr = 0
        self.cc_cnt = 0

    def add(self, eng, fn, r=(), w=(), kind="c"):
        i = len(self.ops)
        deps = set()
        for k in r:
            if k in self.last_w:
                deps.add(self.last_w[k])
        for k in w:
            if k in self.last_w:
                deps.add(self.last_w[k])
            deps.update(self.readers.get(k, ()))
        for k in r:
            self.readers.setdefault(k, []).append(i)
        for k in w:
            self.last_w[k] = i
            self.readers[k] = []
        op = dict(eng=eng, fn=fn, deps=deps, kind=kind, signal=False, sigval=0)
        if kind == "d":
            s = self.asem_rr
            self.asem_rr = (self.asem_rr + 1) % N_ASEM
            op["aprev"] = 16 * self.asem_cnt[s]
            self.asem_cnt[s] += 1
            op["asem"] = s
            op["aval"] = 16 * self.asem_cnt[s]
            self.async_since.append(i)
        elif kind == "cc":
            self.cc_cnt += 1
            op["aval"] = self.cc_cnt
            self.async_since.append(i)
        else:
            self.last_on[eng] = i
        self.ops.append(op)
        return i

    def barrier(self):
        lasts = dict(self.last_on)
        asyncs = list(self.async_since)
        for e in self.ENGS:
            deps = set(asyncs)
            for e2, i in lasts.items():
                if e2 != e:
                    deps.add(i)
            self.ops.append(dict(eng=e, fn=None, deps=deps, kind="b", signal=False, sigval=0))
        self.last_w = {}
        self.readers = {}
        self.async_since = []

    def finalize(self):
        ops = self.ops
        for op in ops:
            for d in op["deps"]:
                dop = ops[d]
                if dop["kind"] != "c":
                    continue
                if dop["eng"] == op["eng"] and op["eng"] == "pe" and op["kind"] in ("c", "b"):
                    continue
                dop["signal"] = True
        cnt = {e: 0 for e in self.ENGS}
        for op in ops:
            if op["kind"] == "c" and op["signal"]:
                cnt[op["eng"]] += 1
                op["sigval"] = cnt[op["eng"]]
        self.cnt = cnt

    def emit(self, eng, e, csem, asems, ccsem):
        ops = self.ops
        waited = {}

        def wait(key, sem, val):
            if val <= 0 or waited.get(key, 0) >= val:
                return
            e.wait_ge(sem, val)
            waited[key] = val

        for op in ops:
            if op["eng"] != eng:
                continue
            for d in sorted(op["deps"]):
                dop = ops[d]
                if dop["kind"] == "d":
                    wait(("a", dop["asem"]), asems[dop["asem"]], dop["aval"])
                elif dop["kind"] == "cc":
                    wait("cc", ccsem, dop["aval"])
                elif dop["kind"] == "c":
                    if not dop["signal"]:
                        continue
                    if dop["eng"] == eng and eng == "pe" and op["kind"] in ("c", "b"):
                        continue
                    wait(("c", dop["eng"]), csem[dop["eng"]], dop["sigval"])
            if op["fn"] is None:
                continue
            if op["kind"] == "d":
                wait(("a", op["asem"]), asems[op["asem"]], op["aprev"])
                op["fn"](e).then_inc(asems[op["asem"]], 16)
            elif op["kind"] == "cc":
                op["fn"](e).then_inc(ccsem)
            else:
                ins = op["fn"](e)
                if op["signal"]:
                    ins.then_inc(csem[eng], 1)


class Arena:
    def __init__(self, ap, nbytes):
        self.ap = ap
        self.n = nbytes
        self.top = 0

    def alloc(self, nbytes, dt=F32):
        off = self.top
        self.top += (nbytes + 63) // 64 * 64
        assert self.top <= self.n, f"arena overflow {self.top} > {self.n}"
        a = self.ap[:, off // 4:(off + nbytes) // 4]
        return a if dt == F32 else a.bitcast(dt)

    def mark(self):
        return self.top

    def release(self, m):
        self.top = m


def build(debug=False):
    nc = bass.Bass("TRN2", target_bir_lowering=False)

    def din(name, shape, dt=F32):
        return nc.dram_tensor(name, list(shape), dt, kind="ExternalInput")

    xo = din("xo", [NT, D])
    hflag_d = din("hflag", [128, 1])
    rc_d = din("rc", [128, 64])
    c_d = din("c", [1, D])
    pos_d = din("pos", [1, TOK], I32)
    invf_d = din("invf", [64, 1])
    w_ada = din("w_ada", [D, 3 * D])
    b_ada = din("b_ada", [1, 3 * D])
    w_in = din("w_in", [D, INW])
    b_gates = din("b_gates", [1, 2 * D])
    qng = din("q_norm_g", [1, 512])
    kvng = din("kv_norm_g", [1, 512])
    wq_d = din("wq", [512, 384])
    wkv_d = din("wkv", [512, 512])
    w_mla_o = din("w_mla_o", [D, D])
    w_pool_g = din("w_pool_g", [4, 256, 256])
    pool_scale = din("pool_scale", [1, 1024])
    w_pool_o = din("w_pool_o", [1024, D])
    w_out = din("w_out", [D, D])
    ln_g = din("ln_g", [1, D])
    ln_b = din("ln_b", [1, D])
    ident_d = din("ident", [128, 128], BF16)
    identf_d = din("identf", [128, 128])
    tri_d = din("tri", [128, 128], BF16)
    out_d = nc.dram_tensor("out", [TOK, D], F32, kind="ExternalOutput")

    lat_in = nc.dram_tensor("lat_in", [LATR, TOK], BF16)
    lat_out = nc.dram_tensor("lat_out", [NCORES * LATR, TOK], BF16)
    cs_in = nc.dram_tensor("cs_in", [128, TOK], F32)
    cs_out = nc.dram_tensor("cs_out", [NCORES * 128, TOK], F32)
    sg_d = nc.dram_tensor("sg_d", [128, 16 * TOK], BF16)
    gm_d = nc.dram_tensor("gm_d", [128, 16 * TOK], BF16)
    zp_d = nc.dram_tensor("zp_d", [128, 16 * TOK], BF16)
    g1_d = nc.dram_tensor("g1_d", [1, D], F32)
    yb_in = [nc.dram_tensor(f"yb_in{g}", [256, GT], F32) for g in range(NGRP)]
    yb_out = nc.dram_tensor("yb_out", [NGRP * D, GT], F32)
    ymine = nc.dram_tensor("ymine", [2 * D, GT], F32)

    dbg = {}
    if debug:
        dbg["lat"] = nc.dram_tensor("dbg_lat", [LATR, TOK], BF16, kind="ExternalOutput")
        dbg["cs"] = nc.dram_tensor("dbg_cs", [128, TOK], F32, kind="ExternalOutput")
        dbg["sg"] = nc.dram_tensor("dbg_sg", [128, 16 * TOK], BF16, kind="ExternalOutput")
        dbg["gm"] = nc.dram_tensor("dbg_gm", [128, 16 * TOK], BF16, kind="ExternalOutput")
        dbg["zp"] = nc.dram_tensor("dbg_zp", [128, 16 * TOK], BF16, kind="ExternalOutput")
        dbg["y"] = nc.dram_tensor("dbg_y", [NGRP * 256, GT], F32, kind="ExternalOutput")

    es = ExitStack()
    with es:
        ARENA_BYTES = 212480
        arena_t = es.enter_context(nc.sbuf_tensor("arena", [128, ARENA_BYTES // 4], F32))
        psum_t = es.enter_context(nc.psum_tensor("psum", [128, 8 * 512], F32))
        csem = {e: es.enter_context(nc.semaphore("cs_" + e)) for e in ("pe", "act", "dve", "pool")}
        asems = [es.enter_context(nc.semaphore(f"as{i}")) for i in range(N_ASEM)]
        ccsem = es.enter_context(nc.semaphore("ccs"))
        A = Arena(arena_t[:, :], ARENA_BYTES)
        ps = [psum_t[:, b * 512:(b + 1) * 512] for b in range(8)]
        R = Rec()
        PID = [None]

        def PS(b):
            return ("ps", b)

        def dma(eng, out, in_, r=(), w=()):
            R.add(eng, lambda e: e.dma_start(out=out, in_=in_), r, w, kind="d")

        def mm(out, lhsT, rhs, start, stop, r, w):
            R.add("pe", lambda e: e.matmul(out, lhsT, rhs, start=start, stop=stop), r, w)

        def act(out, in_, func, r, w, bias=None, scale=None, accum=None):
            kw = {}
            if bias is not None:
                kw["bias"] = bias
            if scale is not None:
                kw["scale"] = scale
            if accum is not None:
                kw["accum_out"] = accum
            R.add("act", lambda e: e.activation(out, in_, func, **kw), r, w)

        def ts2(eng, out, in0, s1, s2, op0, op1, r, w):
            if op1 is None:
                R.add(eng, lambda e: e.tensor_scalar(out, in0, s1, None, op0), r, w)
            else:
                R.add(eng, lambda e: e.tensor_scalar(out, in0, s1, s2, op0, op1), r, w)

        def tt(eng, out, in0, in1, op, r, w):
            R.add(eng, lambda e: e.tensor_tensor(out, in0, in1, op), r, w)

        def stt(eng, out, in0, scalar, in1, op0, op1, r, w):
            R.add(eng, lambda e: e.scalar_tensor_tensor(out, in0, scalar, in1, op0, op1), r, w)

        def cp(eng, out, in_, r, w):
            R.add(eng, lambda e: e.tensor_copy(out, in_), r, w)

        def memset(eng, ap, val, w):
            R.add(eng, lambda e: e.memset(ap, val), (), w)

        ident = A.alloc(256, BF16)
        identf = A.alloc(512, F32)
        tri = A.alloc(256, BF16)
        onesf = A.alloc(512, F32)
        colv = A.alloc(256, F32)
        scb = A.alloc(32, BF16)
        shiftc = A.alloc(64, F32)
        scale1c = A.alloc(64, F32)
        epsc = A.alloc(16, F32)
        hflag = A.alloc(4, F32)
        rc = A.alloc(256, F32)
        invf = A.alloc(4, F32)
        small = A.alloc(64, F32)
        dma("sp", ident, ident_d[:, :], w=["ident"])
        dma("sp", identf, identf_d[:, :], w=["identf"])
        dma("sp", tri, tri_d[:, :], w=["tri"])
        dma("sp", hflag, hflag_d[:, :], w=["hflag"])
        dma("sp", rc, rc_d[:, :], w=["rc"])
        dma("sp", invf[0:64, :], invf_d[:, :], w=["invf"])
        memset("dve", onesf, 1.0, ["onesf"])
        memset("dve", epsc[:, 0:1], 1e-5, ["epsc"])
        memset("dve", epsc[:, 1:2], 1e-6, ["epsc"])
        memset("dve", epsc[:, 2:3], -math.pi, ["epsc"])
        memset("dve", epsc[:, 3:4], 0.0, ["epsc"])

        P_TOP = A.mark()
        hT = A.alloc(16 * NT * 2, BF16).rearrange("p (k t) -> p k t", k=16)
        wbuf = [A.alloc(16 * 512 * 2, BF16).rearrange("p (k n) -> p k n", k=16) for _ in range(2)]
        wb_i = [0]
        M1 = A.mark()

        def load_w(pieces):
            s = wb_i[0] % 2
            wb_i[0] += 1
            for (src, c0, n, dc) in pieces:
                dma("pool", wbuf[s][:, :, dc:dc + n],
                    src[:, c0:c0 + n].rearrange("(k p) n -> p k n", p=128), w=[("wb", s)])
            return s

        sv = A.alloc(512, F32)
        brow = A.alloc(6144 * 4, F32)
        modrow = A.alloc(6144 * 4, F32)
        dma("sp", sv[0:16, :], c_d[:, :].rearrange("o (k p) -> (o k) p", p=128), w=["sv"])
        dma("sp", sv[16:20, :], qng[:, :].rearrange("o (k p) -> (o k) p", p=128), w=["sv"])
        dma("sp", sv[20:24, :], kvng[:, :].rearrange("o (k p) -> (o k) p", p=128), w=["sv"])
        dma("sp", sv[24:32, :], pool_scale[:, :].rearrange("o (k p) -> (o k) p", p=128), w=["sv"])
        dma("sp", sv[32:64, :], b_gates[:, :].rearrange("o (k p) -> (o k) p", p=128), w=["sv"])
        dma("sp", brow[0:1, :], b_ada[:, :], w=["brow"])
        act(sv[0:16, :], sv[0:16, :], AF.Silu, r=["sv"], w=["sv"])
        mm(ps[0][:, 0:64], sv[0:64, :], identf[0:64, 0:64], True, True, r=["sv", "identf"], w=[PS(0)])
        cp("dve", colv, ps[0][:, 0:64], r=[PS(0)], w=["colv"])
        cp("dve", scb, colv[:, 0:16], r=["colv"], w=["scb"])
        qgc, kvgc, pscc, bgc = colv[:, 16:20], colv[:, 20:24], colv[:, 24:32], colv[:, 32:64]
        for n in range(12):
            s = load_w([(w_ada, n * 512, 512, 0)])
            b = 1 + n % 2
            for k in range(16):
                mm(ps[b][0:1, :], scb[:, k:k + 1], wbuf[s][:, k, :], k == 0, k == 15,
                   r=["scb", ("wb", s)], w=[PS(b)])
            tt("dve", modrow[0:1, n * 512:(n + 1) * 512], ps[b][0:1, :], brow[0:1, n * 512:(n + 1) * 512],
               ALU.add, r=[PS(b), "brow"], w=["modrow"])
        for i in range(32):
            mm(ps[3][:, i:i + 1], modrow[0:1, i * 128:(i + 1) * 128], identf[0:1, 0:1], True, True,
               r=["modrow", "identf"], w=[PS(3)])
        cp("dve", shiftc, ps[3][:, 0:16], r=[PS(3)], w=["shiftc"])
        ts2("dve", scale1c, ps[3][:, 16:32], 1.0, None, ALU.add, None, r=[PS(3)], w=["scale1c"])
        ts2("dve", modrow[0:1, 4096:6144], modrow[0:1, 4096:6144], 1.0, None, ALU.add, None,
            r=["modrow"], w=["modrow"])
        dma("sp", g1_d[:, :], modrow[0:1, 4096:6144], r=["modrow"], w=["g1_d"])
        R.barrier()
        A.release(M1)

        xt = [A.alloc(D * 4, F32) for _ in range(2)]
        xn = [A.alloc(D * 2, BF16) for _ in range(2)]
        junk = A.alloc(D * 4, F32)
        for t in range(NT // 128):
            s = t % 2
            sm = small[:, s * 8:(s + 1) * 8]
            SM = ("sm", s)
            dma("sp", xt[s], xo[t * 128:(t + 1) * 128, :], w=[("xt", s)])
            memset("dve", sm[:, 0:3], 0.0, [SM])
            act(junk, xt[s], AF.Identity, r=[("xt", s), SM], w=["junk", SM], accum=sm[:, 0:1])
            ts2("dve", sm[:, 1:2], sm[:, 0:1], -1.0 / D, None, ALU.mult, None, r=[SM], w=[SM])
            act(junk, xt[s], AF.Square, r=[("xt", s), SM], w=["junk", SM], bias=sm[:, 1:2], accum=sm[:, 2:3])
            act(sm[:, 3:4], sm[:, 2:3], AF.Sqrt, r=[SM, "epsc"], w=[SM], bias=epsc[:, 0:1], scale=1.0 / D)
            R.add("dve", lambda e, o=sm[:, 4:5], i=sm[:, 3:4]: e.reciprocal(o, i), [SM], [SM])
            tt("dve", sm[:, 5:6], sm[:, 1:2], sm[:, 4:5], ALU.mult, r=[SM], w=[SM])
            act(xn[s], xt[s], AF.Identity, r=[("xt", s), SM], w=[("xn", s)], bias=sm[:, 5:6], scale=sm[:, 4:5])
            b0 = 2 * (t % 2)
            for k in range(16):
                pb = ps[b0 + k // 8].bitcast(BF16)
                R.add("pe", lambda e, o=pb[:, (k % 8) * 128:(k % 8 + 1) * 128], i=xn[s][:, k * 128:(k + 1) * 128]:
                      e.transpose(o, i, ident), [("xn", s), "ident"], [PS(b0 + k // 8)])
            for k in range(16):
                pb = ps[b0 + k // 8].bitcast(BF16)
                ts2("dve", hT[:, k, t * 128:(t + 1) * 128], pb[:, (k % 8) * 128:(k % 8 + 1) * 128],
                    scale1c[:, k:k + 1], shiftc[:, k:k + 1], ALU.mult, ALU.add,
                    r=[PS(b0 + k // 8), "scale1c", "shiftc"], w=[("hT", t)])
            if t == 0:
                ts2("dve", hT[:, :, 0:128], hT[:, :, 0:128], hflag[:, 0:1], None, ALU.mult, None,
                    r=[("hT", 0), "hflag"], w=[("hT", 0)])
        R.barrier()
        A.release(M1)
        HT_ALL = [("hT", t) for t in range(NT // 128)]

        bank_rr = [0]

        def nbank():
            b = bank_rr[0] % 8
            bank_rr[0] += 1
            return b

        def proj(lhs_fn, M, c0, n, consumer, nk=16, rhs_fn=None, rkeys=None):
            b = nbank()
            for k in range(nk):
                rhs = hT[:, k, c0:c0 + n] if rhs_fn is None else rhs_fn(k)
                mm(ps[b][0:M, 0:n], lhs_fn(k), rhs, k == 0, k == nk - 1,
                   r=(rkeys if rkeys is not None else HT_ALL + [("wb", 0), ("wb", 1)]), w=[PS(b)])
            consumer(b)

        CS = A.alloc(2 * TOK * 4, F32).rearrange("p (c t) -> p c t", c=2)
        posi = A.alloc(TOK * 4, I32)
        ang = A.alloc(TOK * 4, F32)
        rr = A.alloc(TOK * 4, F32)
        dma("sp", posi[0:64, :], pos_d[:, :].partition_broadcast(64), w=["posi"])
        cp("dve", ang[0:64, :], posi[0:64, :], r=["posi"], w=["ang"])
        ts2("dve", ang[0:64, :], ang[0:64, :], invf[0:64, 0:1], None, ALU.mult, None, r=["ang", "invf"], w=["ang"])
        C1 = 6.28125
        C2 = 2 * math.pi - C1
        qi = posi

        def reduce_angle(src_add):
            ts2("dve", rr[0:64, :], ang[0:64, :], src_add, 1.0 / (2 * math.pi), ALU.add, ALU.mult, r=["ang", "CS"], w=["rr"])
            cp("dve", qi[0:64, :], rr[0:64, :], r=["rr"], w=["posi"])
            cp("dve", rr[0:64, :], qi[0:64, :], r=["posi"], w=["rr"])
            stt("dve", t2r[0:64, :], rr[0:64, :], -C1, ang[0:64, :], ALU.mult, ALU.add, r=["rr", "ang"], w=["t2r"])
            stt("dve", t2r[0:64, :], rr[0:64, :], -C2, t2r[0:64, :], ALU.mult, ALU.add, r=["rr", "t2r"], w=["t2r"])
            if src_add != 0.0:
                ts2("dve", t2r[0:64, :], t2r[0:64, :], src_add, None, ALU.add, None, r=["t2r"], w=["t2r"])
            ts2("dve", rr[0:64, :], t2r[0:64, :], math.pi, 2 * math.pi, ALU.is_gt, ALU.mult, r=["t2r"], w=["rr"])
            tt("dve", t2r[0:64, :], t2r[0:64, :], rr[0:64, :], ALU.subtract, r=["t2r", "rr"], w=["t2r"])
            ts2("dve", rr[0:64, :], t2r[0:64, :], -math.pi, 2 * math.pi, ALU.is_lt, ALU.mult, r=["t2r"], w=["rr"])
            tt("dve", rr[0:64, :], t2r[0:64, :], rr[0:64, :], ALU.add, r=["t2r", "rr"], w=["rr"])

        t2r = A.alloc(TOK * 4, F32)
        reduce_angle(0.0)
        act(CS[0:64, 1, :], rr[0:64, :], AF.Sin, r=["rr"], w=["CS"])
        ts2("dve", CS[0:32, 1, :], CS[0:32, 1, :], -1.0, None, ALU.mult, None, r=["CS"], w=["CS"])
        reduce_angle(math.pi / 2)
        act(CS[0:64, 0, :], rr[0:64, :], AF.Sin, r=["rr"], w=["CS"])
        dma("sp", cs_in[0:64, :], CS[0:64, 0, :], r=["CS"], w=["cs_in"])
        dma("sp", cs_in[64:128, :], CS[0:64, 1, :], r=["CS"], w=["cs_in"])

        ql = A.alloc(4 * TOK * 4, F32).rearrange("p (k t) -> p k t", k=4)
        sq = [A.alloc(512 * 4, F32) for _ in range(2)]
        rstd = A.alloc(512 * 4, F32)
        stg = [A.alloc(TOK * 2, BF16) for _ in range(2)]
        stg_i = [0]
        t1 = A.alloc(512 * 4, F32)
        t2 = A.alloc(512 * 4, F32)

        for li, (c0w, gcol, row0) in enumerate(((C_QLAT, qgc, 0), (C_CKV, kvgc, 512))):
            s = load_w([(w_in, c0w, 512, 0)])
            SSQ = 6 + li
            for tt_ in range(4):
                c0 = HALO + tt_ * 512
                for m in range(4):
                    def cons(b, m=m, tt_=tt_):
                        act(ql[:, m, tt_ * 512:(tt_ + 1) * 512], ps[b], AF.Copy, r=[PS(b)], w=[("ql", tt_)])
                        act(sq[m % 2], ps[b], AF.Square, r=[PS(b)], w=[("sq", m % 2)])
                        mm(ps[SSQ], onesf, sq[m % 2], m == 0, m == 3, r=[("sq", m % 2), "onesf"], w=[PS(SSQ)])
                    b = bank_rr[0] % 6
                    bank_rr[0] += 1
                    for k in range(16):
                        mm(ps[b][:, :], wbuf[s][:, k, m * 128:(m + 1) * 128], hT[:, k, c0:c0 + 512],
                           k == 0, k == 15, r=HT_ALL + [("wb", s)], w=[PS(b)])
                    cons(b)
                act(rstd, ps[SSQ], AF.Sqrt, r=[PS(SSQ), "epsc"], w=["rstd"], bias=epsc[:, 1:2], scale=1.0 / 512)
                R.add("dve", lambda e: e.reciprocal(rstd, rstd), ["rstd"], ["rstd"])
                for m in range(4):
                    si = stg_i[0] % 2
                    stg_i[0] += 1
                    stt("dve", stg[si][:, 0:512], ql[:, m, tt_ * 512:(tt_ + 1) * 512], gcol[:, m:m + 1], rstd,
                        ALU.mult, ALU.mult, r=[("ql", tt_), "rstd", "colv"], w=[("stg", si)])
                    dma("sp", lat_in[row0 + m * 128:row0 + (m + 1) * 128, tt_ * 512:(tt_ + 1) * 512],
                        stg[si][:, 0:512], r=[("stg", si)], w=[("lat_in", row0 + m * 128, tt_)])
        s = load_w([(w_in, C_KPE, 64, 0), (w_in, C_KPE + 32, 32, 64), (w_in, C_KPE, 32, 96)])
        for tt_ in range(4):
            c0 = HALO + tt_ * 512
            bA = bank_rr[0] % 6
            bB = (bank_rr[0] + 1) % 6
            bank_rr[0] += 2
            for k in range(16):
                mm(ps[bA][0:64, :], wbuf[s][:, k, 0:64], hT[:, k, c0:c0 + 512], k == 0, k == 15,
                   r=HT_ALL + [("wb", s)], w=[PS(bA)])
            for k in range(16):
                mm(ps[bB][0:64, :], wbuf[s][:, k, 64:128], hT[:, k, c0:c0 + 512], k == 0, k == 15,
                   r=HT_ALL + [("wb", s)], w=[PS(bB)])
            tt("dve", t1[0:64, :], ps[bA][0:64, :], CS[0:64, 0, tt_ * 512:(tt_ + 1) * 512], ALU.mult,
               r=[PS(bA), "CS"], w=["t1"])
            tt("dve", t2[0:64, :], ps[bB][0:64, :], CS[0:64, 1, tt_ * 512:(tt_ + 1) * 512], ALU.mult,
               r=[PS(bB), "CS"], w=["t2"])
            si = stg_i[0] % 2
            stg_i[0] += 1
            tt("dve", stg[si][0:64, 0:512], t1[0:64, :], t2[0:64, :], ALU.add, r=["t1", "t2"], w=[("stg", si)])
            dma("sp", lat_in[1024:1088, tt_ * 512:(tt_ + 1) * 512], stg[si][0:64, 0:512],
                r=[("stg", si)], w=[("lat_in", 1024, tt_)])
        R.add("pool", lambda e: e.collective_compute("AllGather", ALU.bypass, replica_groups=[list(range(NCORES))],
                                                     ins=[lat_in.ap().opt()], outs=[lat_out.ap().opt()]),
              [("lat_in", r0, t_) for r0 in range(0, 1088, 128) for t_ in range(4)], ["lat_out"], kind="cc")
        R.add("pool", lambda e: e.collective_compute("AllGather", ALU.bypass, replica_groups=[list(range(NCORES))],
                                                     ins=[cs_in.ap().opt()], outs=[cs_out.ap().opt()]),
              ["cs_in"], ["cs_out"], kind="cc")
        if debug:
            dma("sp", dbg["lat"][:, :], lat_in[:, :],
                r=[("lat_in", r0, t_) for r0 in range(0, 1088, 128) for t_ in range(4)], w=["dbg_lat"])
            dma("sp", dbg["cs"][:, :], cs_in[:, :], r=["cs_in"], w=["dbg_cs"])
        R.barrier()
        A.release(M1)

        stg = [A.alloc(TOK * 2, BF16) for _ in range(2)]
        for (cbase, func, dst, dname, bias_c0) in ((C_MG, AF.Silu, sg_d, "sg_d", None),
                                                   (C_GL, AF.Sigmoid, gm_d, "gm_d", 0)):
            for grp in range(4):
                s = load_w([(w_in, cbase + grp * 512, 512, 0)])
                for m4 in range(4):
                    m = grp * 4 + m4
                    si = stg_i[0] % 2
                    stg_i[0] += 1
                    for tt_ in range(4):
                        def cons(b, tt_=tt_, si=si, m=m):
                            if bias_c0 is None:
                                act(stg[si][:, tt_ * 512:(tt_ + 1) * 512], ps[b], func, r=[PS(b)], w=[("stg", si)])
                            else:
                                act(stg[si][:, tt_ * 512:(tt_ + 1) * 512], ps[b], func, r=[PS(b), "colv"],
                                    w=[("stg", si)], bias=bgc[:, bias_c0 + m:bias_c0 + m + 1])
                        proj(lambda k, s=s, m4=m4: wbuf[s][:, k, m4 * 128:(m4 + 1) * 128], 128,
                             HALO + tt_ * 512, 512, cons, rkeys=HT_ALL + [("wb", s)])
                    dma("sp", dst[:, m * TOK:(m + 1) * TOK], stg[si], r=[("stg", si)], w=[(dname, m)])

        wpg = A.alloc(8 * 256 * 2, BF16).rearrange("p (j n) -> p j n", j=8)
        for g in range(4):
            for ic in range(2):
                dma("pool", wpg[:, g * 2 + ic, :], w_pool_g[g, ic * 128:(ic + 1) * 128, :], w=["wpg"])
        u = A.alloc(NT * 4, F32)
        sA = A.alloc(NT * 4, F32)
        sB = A.alloc(NT * 4, F32)
        pl = A.alloc(2 * TOK * 2, BF16).rearrange("p (c t) -> p c t", c=2)
        sgp = A.alloc(2 * TOK * 2, BF16).rearrange("p (c t) -> p c t", c=2)
        pm = A.alloc(8 * TOK * 2, BF16).rearrange("p (k t) -> p k t", k=8)
        tmp16 = A.alloc(64, F32)
        for g in range(4):
            wdw = 2 ** (g + 1)
            s = load_w([(w_in, C_PI + g * 256, 256, 0), (w_in, C_PG + g * 256, 256, 256)])
            for c in range(2):
                def cons_h(b):
                    act(u[:, 0:128], ps[b][:, 0:128], AF.Copy, r=[PS(b)], w=["u"])
                proj(lambda k, s=s, c=c: wbuf[s][:, k, c * 128:(c + 1) * 128], 128, 0, 128, cons_h,
                     rkeys=HT_ALL + [("wb", s)])
                for tt_ in range(4):
                    def cons(b, tt_=tt_):
                        act(u[:, HALO + tt_ * 512:HALO + (tt_ + 1) * 512], ps[b], AF.Copy, r=[PS(b)], w=["u"])
                    proj(lambda k, s=s, c=c: wbuf[s][:, k, c * 128:(c + 1) * 128], 128, HALO + tt_ * 512, 512,
                         cons, rkeys=HT_ALL + [("wb", s)])
                tt("dve", sA[:, 1:NT], u[:, 1:NT], u[:, 0:NT - 1], ALU.add, r=["u"], w=["sA"])
                cur, oth, ck, ok_ = sA, sB, "sA", "sB"
                sh = 2
                while sh < wdw:
                    lo = 2 * sh - 1
                    tt("dve", oth[:, lo:NT], cur[:, lo:NT], cur[:, lo - sh:NT - sh], ALU.add, r=[ck], w=[ok_])
                    cur, oth, ck, ok_ = oth, cur, ok_, ck
                    sh *= 2
                stt("dve", pl[:, c, :], cur[:, HALO:NT], 1.0 / wdw, u[:, HALO:NT], ALU.mult, ALU.subtract,
                    r=[ck, "u"], w=["pl"])
                tt("dve", tmp16[:, 0:16], cur[:, HALO:HALO + 16], rc[:, g * 16:(g + 1) * 16], ALU.mult,
                   r=[ck, "rc"], w=["tmp16"])
                tt("dve", pl[:, c, 0:16], tmp16[:, 0:16], u[:, HALO:HALO + 16], ALU.subtract,
                   r=["tmp16", "u", "pl"], w=["pl"])
                for tt_ in range(4):
                    def cons(b, tt_=tt_, c=c):
                        act(sgp[:, c, tt_ * 512:(tt_ + 1) * 512], ps[b], AF.Silu, r=[PS(b)], w=["sgp"])
                    proj(lambda k, s=s, c=c: wbuf[s][:, k, 256 + c * 128:256 + (c + 1) * 128], 128,
                         HALO + tt_ * 512, 512, cons, rkeys=HT_ALL + [("wb", s)])
            for oc in range(2):
                for tt_ in range(4):
                    def cons(b, tt_=tt_, oc=oc, g=g):
                        stt("dve", pm[:, g * 2 + oc, tt_ * 512:(tt_ + 1) * 512], ps[b],
                            pscc[:, g * 2 + oc:g * 2 + oc + 1], sgp[:, oc, tt_ * 512:(tt_ + 1) * 512],
                            ALU.mult, ALU.mult, r=[PS(b), "sgp", "colv"], w=["pm"])
                    proj(lambda ic, g=g, oc=oc: wpg[:, g * 2 + ic, oc * 128:(oc + 1) * 128], 128, 0, 512, cons,
                         nk=2, rhs_fn=lambda ic, tt_=tt_: pl[:, ic, tt_ * 512:(tt_ + 1) * 512],
                         rkeys=["wpg", "pl"])
        wpo = [A.alloc(8 * 512 * 2, BF16).rearrange("p (k n) -> p k n", k=8) for _ in range(1)]
        gp = [A.alloc(512 * 4, F32) for _ in range(2)]
        gp_i = 0
        for grp in range(4):
            s = load_w([(w_in, C_GL + D + grp * 512, 512, 0)])
            ws = 0
            dma("pool", wpo[ws], w_pool_o[:, grp * 512:(grp + 1) * 512].rearrange("(k p) n -> p k n", p=128),
                w=[("wpo", ws)])
            for m4 in range(4):
                m = grp * 4 + m4
                si = stg_i[0] % 2
                stg_i[0] += 1
                for tt_ in range(4):
                    gi = gp_i % 2
                    gp_i += 1

                    def cons_g(b, gi=gi, m=m):
                        act(gp[gi], ps[b], AF.Sigmoid, r=[PS(b), "colv"], w=[("gp", gi)],
                            bias=bgc[:, 16 + m:16 + m + 1])
                    proj(lambda k, s=s, m4=m4: wbuf[s][:, k, m4 * 128:(m4 + 1) * 128], 128,
                         HALO + tt_ * 512, 512, cons_g, rkeys=HT_ALL + [("wb", s)])

                    def cons_z(b, gi=gi, si=si, tt_=tt_):
                        tt("dve", stg[si][:, tt_ * 512:(tt_ + 1) * 512], ps[b], gp[gi], ALU.mult,
                           r=[PS(b), ("gp", gi)], w=[("stg", si)])
                    proj(lambda k8, ws=ws, m4=m4: wpo[ws][:, k8, m4 * 128:(m4 + 1) * 128], 128, 0, 512, cons_z,
                         nk=8, rhs_fn=lambda k8, tt_=tt_: pm[:, k8, tt_ * 512:(tt_ + 1) * 512],
                         rkeys=["pm", ("wpo", ws)])
                dma("sp", zp_d[:, m * TOK:(m + 1) * TOK], stg[si], r=[("stg", si)], w=[("zp_d", m)])
        R.barrier()
        A.release(P_TOP)
        if debug:
            dma("sp", dbg["sg"][:, :], sg_d[:, :], w=["dbg_sg"])
            dma("sp", dbg["gm"][:, :], gm_d[:, :], w=["dbg_gm"])
            dma("sp", dbg["zp"][:, :], zp_d[:, :], w=["dbg_zp"])

        KT = A.alloc(2 * S * 2, BF16).rearrange("p (a t) -> p a t", a=2)
        V = A.alloc(128 * 256 * 2, BF16).rearrange("p (b n) -> p b n", b=128)
        KPE = A.alloc(S * 2, BF16)
        wq = A.alloc(4 * 512 * 2, BF16).rearrange("p (k n) -> p k n", k=4)
        wkv = A.alloc(4 * 512 * 2, BF16).rearrange("p (k n) -> p k n", k=4)
        qlat = [A.alloc(4 * 512 * 2, BF16).rearrange("p (k n) -> p k n", k=4) for _ in range(1)]
        ckv = [A.alloc(4 * 512 * 2, BF16).rearrange("p (k n) -> p k n", k=4) for _ in range(1)]
        cst = [A.alloc(2 * 512 * 4, F32).rearrange("p (c n) -> p c n", c=2) for _ in range(1)]
        QN = A.alloc(2 * 512 * 2, BF16).rearrange("p (a n) -> p a n", a=2)
        QR = A.alloc(512 * 2, BF16)
        NPT = 4
        pT = [A.alloc(512 * 2, BF16) for _ in range(NPT)]
        acc = A.alloc(2 * 512 * 4, F32).rearrange("p (a n) -> p a n", a=2)
        rcp = A.alloc(512 * 4, F32)
        yT = [A.alloc(512 * 4, F32) for _ in range(2)]

        def kmaj(src_rows):
            return src_rows.rearrange("(k p) n -> p k n", p=128)

        for (c0, n, dc) in ((0, 128, 0), (192, 128, 128), (128, 64, 256), (320, 64, 320),
                            (160, 32, 384), (128, 32, 416), (352, 32, 448), (320, 32, 480)):
            dma("pool", wq[:, :, dc:dc + n], kmaj(wq_d[:, c0:c0 + n]), w=["wq"])
        for (c0, n, dc) in ((0, 128, 0), (256, 128, 128), (128, 128, 256), (384, 128, 384)):
            dma("pool", wkv[:, :, dc:dc + n], kmaj(wkv_d[:, c0:c0 + n]), w=["wkv"])

        step_i = [0]
        yt_i = [0]

        def load_tile(j):
            s = 0
            rk, t0 = j // 4, (j % 4) * 512
            base = rk * LATR
            dma("sp", qlat[s], kmaj(lat_out[base:base + 512, t0:t0 + 512]), r=["lat_out"], w=[("qlat", s)])
            dma("sp", ckv[s], kmaj(lat_out[base + 512:base + 1024, t0:t0 + 512]), r=["lat_out"], w=[("ckv", s)])
            for h in range(2):
                dma("sp", KPE[h * 64:(h + 1) * 64, j * 512:(j + 1) * 512],
                    lat_out[base + 1024:base + 1088, t0:t0 + 512], r=["lat_out"], w=[("KPE", j)])
                for c in range(2):
                    dma("sp", cst[s][h * 64:(h + 1) * 64, c, :],
                        cs_out[rk * 128 + c * 64:rk * 128 + (c + 1) * 64, t0:t0 + 512],
                        r=["cs_out"], w=[("cst", s)])

        load_tile(0)
        for j in range(NQT):
            s = 0
            for a in range(2):
                for k in range(4):
                    mm(ps[5 + a], wkv[:, k, a * 128:(a + 1) * 128], ckv[s][:, k, :], k == 0, k == 3,
                       r=["wkv", ("ckv", s)], w=[PS(5 + a)])
                act(KT[:, a, j * 512:(j + 1) * 512], ps[5 + a], AF.Copy, r=[PS(5 + a)], w=[("KT", j)])
            for half in range(2):
                for tb2 in range(2):
                    tb = half * 2 + tb2
                    for k in range(4):
                        mm(ps[7][:, tb2 * 256:(tb2 + 1) * 256], ckv[s][:, k, tb * 128:(tb + 1) * 128],
                           wkv[:, k, 256:512], k == 0, k == 3, r=["wkv", ("ckv", s)], w=[PS(7)])
                cp("dve", V[:, 4 * j + half * 2:4 * j + half * 2 + 2, :],
                   ps[7].rearrange("p (b n) -> p b n", b=2), r=[PS(7)], w=[("V", j)])
            for a in range(2):
                for k in range(4):
                    mm(ps[5 + a], wq[:, k, a * 128:(a + 1) * 128], qlat[s][:, k, :], k == 0, k == 3,
                       r=["wq", ("qlat", s)], w=[PS(5 + a)])
                act(QN[:, a, :], ps[5 + a], AF.Copy, r=[PS(5 + a)], w=["QN"])
            for v_ in range(2):
                for k in range(4):
                    mm(ps[5 + v_], wq[:, k, 256 + v_ * 128:256 + (v_ + 1) * 128], qlat[s][:, k, :], k == 0, k == 3,
                       r=["wq", ("qlat", s)], w=[PS(5 + v_)])
            tt("dve", yT[0], ps[5], cst[s][:, 0, :], ALU.mult, r=[PS(5), ("cst", s)], w=[("yT", 0)])
            tt("dve", rcp, ps[6], cst[s][:, 1, :], ALU.mult, r=[PS(6), ("cst", s)], w=["rcp"])
            tt("dve", QR, yT[0], rcp, ALU.add, r=[("yT", 0), "rcp"], w=["QR"])
            if j + 1 < NQT:
                load_tile(j + 1)

            nkb = 4 * j + 4
            steps = [(kb, a) for kb in range(nkb) for a in range(2)]
            info = {}

            def issue_S(idx):
                kb, a = steps[idx]
                q0 = max(0, kb - 4 * j) * 128
                g_i = step_i[0]
                step_i[0] += 1
                sb, slot = g_i % 3, g_i % NPT
                info[idx] = (q0, slot)
                mm(ps[sb][:, q0:512], KT[:, a, kb * 128:(kb + 1) * 128], QN[:, a, q0:512], True, False,
                   r=[("KT", kb // 4), "QN"], w=[PS(sb)])
                mm(ps[sb][:, q0:512], KPE[a * 64:(a + 1) * 64, kb * 128:(kb + 1) * 128],
                   QR[a * 64:(a + 1) * 64, q0:512], False, True, r=[("KPE", kb // 4), "QR"], w=[PS(sb)])
                act(pT[slot][:, q0:512], ps[sb][:, q0:512], AF.Exp, r=[PS(sb)], w=[("pT", slot)], scale=SCALE)
                if kb >= 4 * j:
                    tt("dve", pT[slot][:, q0:q0 + 128], pT[slot][:, q0:q0 + 128], tri, ALU.mult,
                       r=[("pT", slot), "tri"], w=[("pT", slot)])
                if kb == 0:
                    cp("dve", acc[:, a, :], pT[slot], r=[("pT", slot)], w=[("acc", a)])
                else:
                    tt("dve", acc[:, a, q0:512], acc[:, a, q0:512], pT[slot][:, q0:512], ALU.add,
                       r=[("pT", slot), ("acc", a)], w=[("acc", a)])

            def issue_PV(idx):
                kb, a = steps[idx]
                q0, slot = info[idx]
                mm(ps[3 + a][:, q0:512], V[:, kb, a * 128:(a + 1) * 128], pT[slot][:, q0:512],
                   kb == 0, kb == nkb - 1, r=[("V", kb // 4), ("pT", slot)], w=[PS(3 + a)])

            LOOK = 2
            for idx in range(len(steps) + LOOK):
                if idx < len(steps):
                    issue_S(idx)
                if idx >= LOOK:
                    issue_PV(idx - LOOK)
            g = j // 2
            for a in range(2):
                mm(ps[5 + a], onesf, acc[:, a, :], True, True, r=[("acc", a), "onesf"], w=[PS(5 + a)])
                R.add("dve", lambda e, i=ps[5 + a]: e.reciprocal(rcp, i), [PS(5 + a)], ["rcp"])
                yi = yt_i[0] % 2
                yt_i[0] += 1
                tt("dve", yT[yi], ps[3 + a], rcp, ALU.mult, r=[PS(3 + a), "rcp"], w=[("yT", yi)])
                dma("sp", yb_in[g][a * 128:(a + 1) * 128, (j % 2) * 512:(j % 2 + 1) * 512], yT[yi],
                    r=[("yT", yi)], w=[("yb_in", g)])
            if j % 2 == 1:
                R.add("pool", lambda e, g=g: e.collective_compute(
                    "AllGather", ALU.bypass, replica_groups=[list(range(NCORES))],
                    ins=[yb_in[g].ap().opt()], outs=[yb_out[g * D:(g + 1) * D, :].opt()]),
                    [("yb_in", g)], ["yb_out"], kind="cc")
                if debug:
                    dma("sp", dbg["y"][g * 256:(g + 1) * 256, :], yb_in[g][:, :], r=[("yb_in", g)], w=["dbg_y"])
        R.barrier()
        A.release(P_TOP)

        zall = A.alloc(16 * TOK * 2, BF16).rearrange("p (k t) -> p k t", k=16)
        M3 = A.mark()
        wmo = A.alloc(16 * D * 2, BF16).rearrange("p (k n) -> p k n", k=16)
        ym = A.alloc(16 * 512 * 2, BF16).rearrange("p (k n) -> p k n", k=16)
        yat = [A.alloc(512 * 4, F32) for _ in range(3)]
        sgt = [A.alloc(512 * 2, BF16) for _ in range(3)]
        gmt = [A.alloc(512 * 2, BF16) for _ in range(2)]
        zpt = [A.alloc(512 * 2, BF16) for _ in range(2)]
        zt = A.alloc(512 * 4, F32)
        for q4 in range(4):
            dma("pool", wmo[:, :, q4 * 512:(q4 + 1) * 512], kmaj(w_mla_o[:, q4 * 512:(q4 + 1) * 512]), w=["wmo"])
        R.add("sp", lambda e: e.dma_start(out=ymine[:, :], in_=yb_out[bass.ts(PID[0], 2 * D), :]),
              ["yb_out"], ["ymine"], kind="d")
        li = 0
        for tt_ in range(4):
            tcol = (tt_ % 2) * 512
            for k in range(16):
                s3 = li % 3
                li += 1
                r0 = (tt_ // 2) * D + k * 128
                dma("sp", yat[s3], ymine[r0:r0 + 128, tcol:tcol + 512], r=["ymine"], w=[("yat", s3)])
                dma("sp", sgt[s3], sg_d[:, k * TOK + tt_ * 512:k * TOK + (tt_ + 1) * 512], r=["sg_d"], w=[("sgt", s3)])
                tt("dve", ym[:, k, :], yat[s3], sgt[s3], ALU.mult, r=[("yat", s3), ("sgt", s3)], w=["ym"])
            for m in range(16):
                s2 = m % 2
                dma("sp", gmt[s2], gm_d[:, m * TOK + tt_ * 512:m * TOK + (tt_ + 1) * 512], r=["gm_d"], w=[("gmt", s2)])
                dma("sp", zpt[s2], zp_d[:, m * TOK + tt_ * 512:m * TOK + (tt_ + 1) * 512], r=["zp_d"], w=[("zpt", s2)])
                b = nbank()
                for k in range(16):
                    mm(ps[b], wmo[:, k, m * 128:(m + 1) * 128], ym[:, k, :], k == 0, k == 15, r=["wmo", "ym"], w=[PS(b)])
                tt("dve", zt, ps[b], gmt[s2], ALU.mult, r=[PS(b), ("gmt", s2)], w=["zt"])
                tt("dve", zall[:, m, tt_ * 512:(tt_ + 1) * 512], zt, zpt[s2], ALU.add, r=["zt", ("zpt", s2)], w=["zall"])
        R.barrier()
        A.release(M3)
        wout = A.alloc(16 * D * 2, BF16).rearrange("p (k n) -> p k n", k=16)
        g1b = A.alloc(D * 4, F32)
        lgb = A.alloc(D * 4, F32)
        lbb = A.alloc(D * 4, F32)
        xt = [A.alloc(D * 4, F32) for _ in range(2)]
        ub = A.alloc(D * 4, F32)
        ob = [A.alloc(D * 4, F32) for _ in range(2)]
        junk = A.alloc(D * 4, F32)
        for q4 in range(4):
            dma("pool", wout[:, :, q4 * 512:(q4 + 1) * 512], kmaj(w_out[:, q4 * 512:(q4 + 1) * 512]), w=["wout"])
        dma("sp", g1b, g1_d[:, :].partition_broadcast(128), r=["g1_d"], w=["g1b"])
        dma("sp", lgb, ln_g[:, :].partition_broadcast(128), w=["lgb"])
        dma("sp", lbb, ln_b[:, :].partition_broadcast(128), w=["lbb"])
        for tb in range(TOK // 128):
            s = tb % 2
            sm = small[:, s * 8:(s + 1) * 8]
            SM = ("sm", s)
            dma("sp", xt[s], xo[HALO + tb * 128:HALO + (tb + 1) * 128, :], w=[("xt", s)])
            hb = 4 * (tb % 2)
            for n4 in range(4):
                for k in range(16):
                    mm(ps[hb + n4], zall[:, k, tb * 128:(tb + 1) * 128], wout[:, k, n4 * 512:(n4 + 1) * 512],
                       k == 0, k == 15, r=["zall", "wout"], w=[PS(hb + n4)])
            for n4 in range(4):
                tt("dve", ub[:, n4 * 512:(n4 + 1) * 512], ps[hb + n4], g1b[:, n4 * 512:(n4 + 1) * 512], ALU.mult,
                   r=[PS(hb + n4), "g1b"], w=["ub"])
            stt("dve", ub, xt[s], ALPHA, ub, ALU.mult, ALU.add, r=[("xt", s), "ub"], w=["ub"])
            memset("dve", sm[:, 0:3], 0.0, [SM])
            act(junk, ub, AF.Identity, r=["ub", SM], w=["junk", SM], accum=sm[:, 0:1])
            ts2("dve", sm[:, 1:2], sm[:, 0:1], -1.0 / D, None, ALU.mult, None, r=[SM], w=[SM])
            act(junk, ub, AF.Square, r=["ub", SM], w=["junk", SM], bias=sm[:, 1:2], accum=sm[:, 2:3])
            act(sm[:, 3:4], sm[:, 2:3], AF.Sqrt, r=[SM, "epsc"], w=[SM], bias=epsc[:, 0:1], scale=1.0 / D)
            R.add("dve", lambda e, o=sm[:, 4:5], i=sm[:, 3:4]: e.reciprocal(o, i), [SM], [SM])
            tt("dve", sm[:, 5:6], sm[:, 1:2], sm[:, 4:5], ALU.mult, r=[SM], w=[SM])
            act(ob[s], ub, AF.Identity, r=["ub", SM], w=[("ob", s)], bias=sm[:, 5:6], scale=sm[:, 4:5])
            tt("pool", ob[s], ob[s], lgb, ALU.mult, r=[("ob", s), "lgb"], w=[("ob", s)])
            tt("pool", ob[s], ob[s], lbb, ALU.add, r=[("ob", s), "lbb"], w=[("ob", s)])
            dma("sp", out_d[tb * 128:(tb + 1) * 128, :], ob[s], r=[("ob", s)], w=["out"])
        R.barrier()

        R.finalize()
        with nc.Block() as block:
            @block.tensor
            def _(e):
                R.emit("pe", e, csem, asems, ccsem)

            @block.scalar
            def _(e):
                R.emit("act", e, csem, asems, ccsem)

            @block.vector
            def _(e):
                R.emit("dve", e, csem, asems, ccsem)

            @block.gpsimd
            def _(e):
                R.emit("pool", e, csem, asems, ccsem)

            @block.sync
            def _(e):
                PID[0] = nc.partition_id([mybir.EngineType.SP])
                R.emit("sp", e, csem, asems, ccsem)
    return nc


def make_in_maps(inputs):
    x = np.asarray(inputs["x"], np.float32)[0]
    pos = np.asarray(inputs["positions"], np.int32)
    g = lambda k: np.ascontiguousarray(np.asarray(inputs[k], np.float32)[0])
    wqb, wkvb = g("w_q_b"), g("w_kv_b")
    invf = (np.float32(10000.0) ** (-np.arange(0, 64, 2, dtype=np.float32) / np.float32(64))).astype(np.float32)
    invf = np.concatenate([invf, invf]).reshape(64, 1)
    ident = np.eye(128, dtype=np.float32)
    tri = (np.arange(128)[:, None] <= np.arange(128)[None, :]).astype(np.float32)
    common = dict(
        c=np.asarray(inputs["c"], np.float32), invf=invf, w_ada=g("w_ada"),
        b_ada=np.asarray(inputs["b_ada"], np.float32), w_in=g("w_in"),
        b_gates=np.asarray(inputs["b_gates"], np.float32), q_norm_g=np.asarray(inputs["q_norm_g"], np.float32),
        kv_norm_g=np.asarray(inputs["kv_norm_g"], np.float32), w_mla_o=g("w_mla_o"), w_pool_g=g("w_pool_g"),
        pool_scale=np.asarray(inputs["pool_scale"], np.float32), w_pool_o=g("w_pool_o"), w_out=g("w_out"),
        ln_g=np.asarray(inputs["ln_g"], np.float32), ln_b=np.asarray(inputs["ln_b"], np.float32),
        ident=ident.astype(ml_dtypes.bfloat16), identf=ident, tri=tri.astype(ml_dtypes.bfloat16),
    )
    maps = []
    for c in range(NCORES):
        t0 = c * TOK
        xo = np.zeros((NT, D), np.float32)
        if c > 0:
            xo[:HALO] = x[t0 - HALO:t0]
        xo[HALO:] = x[t0:t0 + TOK]
        rc = np.zeros((128, 64), np.float32)
        for gi, w in enumerate((2, 4, 8, 16)):
            tglob = t0 + np.arange(16)
            rc[:, gi * 16:(gi + 1) * 16] = (1.0 / np.minimum(tglob + 1, w)).astype(np.float32)[None, :]
        m = dict(common)
        m.update(
            xo=xo, hflag=np.full((128, 1), 0.0 if c == 0 else 1.0, np.float32), rc=rc,
            pos=np.ascontiguousarray(pos[:, t0:t0 + TOK]),
            wq=np.ascontiguousarray(wqb[:, c * 384:(c + 1) * 384]),
            wkv=np.ascontiguousarray(wkvb[:, c * 512:(c + 1) * 512]),
        )
        maps.append(m)
    return maps


_NC_CACHE = {}


def kernel(**inputs):
    if "nc" not in _NC_CACHE:
        _NC_CACHE["nc"] = build()
    nc = _NC_CACHE["nc"]
    maps = make_in_maps(inputs)
    res = run_bass_kernel_spmd(nc, maps, core_ids=list(range(NCORES)))
    out = np.concatenate([np.asarray(res.results[c]["out"], np.float32) for c in range(NCORES)], axis=0)
    return out.reshape(1, S, D)
```

```python
import math
from contextlib import ExitStack

import ml_dtypes
import numpy as np

import concourse.bass as bass
import concourse.mybir as mybir
from concourse.bass_utils import run_bass_kernel_spmd

F32 = mybir.dt.float32
BF16 = mybir.dt.bfloat16
I32 = mybir.dt.int32
ALU = mybir.AluOpType
AF = mybir.ActivationFunctionType

NCORES = 8
S = 16384
D = 2048
TOK = S // NCORES
HALO = 128
NT = TOK + HALO
INW = 9280
C_QLAT, C_CKV, C_KPE, C_MG, C_PI, C_PG, C_GL = 0, 512, 1024, 1088, 3136, 4160, 5184
LATR = 1088
SCALE = 192.0 ** -0.5
ALPHA = 2.0 ** 0.25
NQT = S // 512
NGRP = 16
GT = S // NGRP
N_ASEM = 28
N_ASEM_SW = 8


class Rec:
    ENGS = ("pe", "act", "dve", "pool", "sp")

    def __init__(self):
        self.ops = []
        self.last_w = {}
        self.readers = {}
        self.last_on = {}
        self.async_since = []
        self.asem_cnt = [0] * N_ASEM
        self.asem_rr = 0
        self.asem_rr_sw = 0
        self.cc_cnt = 0

    def add(self, eng, fn, r=(), w=(), kind="c"):
        i = len(self.ops)
        deps = set()
        for k in r:
            if k in self.last_w:
                deps.add(self.last_w[k])
        for k in w:
            if k in self.last_w:
                deps.add(self.last_w[k])
            deps.update(self.readers.get(k, ()))
        for k in r:
            self.readers.setdefault(k, []).append(i)
        for k in w:
            self.last_w[k] = i
            self.readers[k] = []
        op = dict(eng=eng, fn=fn, deps=deps, kind=kind, signal=False, sigval=0)
        if kind == "d":
            if eng == "pool":
                s = self.asem_rr_sw
                self.asem_rr_sw = (self.asem_rr_sw + 1) % N_ASEM_SW
            else:
                s = N_ASEM_SW + self.asem_rr
                self.asem_rr = (self.asem_rr + 1) % (N_ASEM - N_ASEM_SW)
            op["aprev"] = 16 * self.asem_cnt[s]
            self.asem_cnt[s] += 1
            op["asem"] = s
            op["aval"] = 16 * self.asem_cnt[s]
            self.async_since.append(i)
        elif kind == "cc":
            self.cc_cnt += 1
            op["aval"] = self.cc_cnt
            self.async_since.append(i)
        else:
            self.last_on[eng] = i
        self.ops.append(op)
        return i

    def barrier(self):
        lasts = dict(self.last_on)
        asyncs = list(self.async_since)
        for e in self.ENGS:
            deps = set(asyncs)
            for e2, i in lasts.items():
                if e2 != e:
                    deps.add(i)
            self.ops.append(dict(eng=e, fn=None, deps=deps, kind="b", signal=False, sigval=0))
        self.last_w = {}
        self.readers = {}
        self.async_since = []

    def finalize(self):
        ops = self.ops
        for op in ops:
            for d in op["deps"]:
                dop = ops[d]
                if dop["kind"] != "c":
                    continue
                if dop["eng"] == op["eng"] and op["eng"] == "pe" and op["kind"] in ("c", "b"):
                    continue
                dop["signal"] = True
        cnt = {e: 0 for e in self.ENGS}
        for op in ops:
            if op["kind"] == "c" and op["signal"]:
                cnt[op["eng"]] += 1
                op["sigval"] = cnt[op["eng"]]
        self.cnt = cnt

    def emit(self, eng, e, csem, asems, ccsem):
        ops = self.ops
        waited = {}

        def wait(key, sem, val):
            if val <= 0 or waited.get(key, 0) >= val:
                return
            e.wait_ge(sem, val)
            waited[key] = val

        for op in ops:
            if op["eng"] != eng:
                continue
            for d in sorted(op["deps"]):
                dop = ops[d]
                if dop["kind"] == "d":
                    wait(("a", dop["asem"]), asems[dop["asem"]], dop["aval"])
                elif dop["kind"] == "cc":
                    wait("cc", ccsem, dop["aval"])
                elif dop["kind"] == "c":
                    if not dop["signal"]:
                        continue
                    if dop["eng"] == eng and eng == "pe" and op["kind"] in ("c", "b"):
                        continue
                    wait(("c", dop["eng"]), csem[dop["eng"]], dop["sigval"])
            if op["fn"] is None:
                continue
            if op["kind"] == "d":
                wait(("a", op["asem"]), asems[op["asem"]], op["aprev"])
                op["fn"](e).then_inc(asems[op["asem"]], 16)
            elif op["kind"] == "cc":
                op["fn"](e).then_inc(ccsem)
            else:
                ins = op["fn"](e)
                if op["signal"]:
                    ins.then_inc(csem[eng], 1)


class Arena:
    def __init__(self, ap, nbytes):
        self.ap = ap
        self.n = nbytes
        self.top = 0

    def alloc(self, nbytes, dt=F32):
        off = self.top
        self.top += (nbytes + 63) // 64 * 64
        assert self.top <= self.n, f"arena overflow {self.top} > {self.n}"
        a = self.ap[:, off // 4:(off + nbytes) // 4]
        return a if dt == F32 else a.bitcast(dt)

    def mark(self):
        return self.top

    def release(self, m):
        self.top = m


def build(debug=False):
    nc = bass.Bass("TRN2", target_bir_lowering=False)

    def din(name, shape, dt=F32):
        return nc.dram_tensor(name, list(shape), dt, kind="ExternalInput")

    xo = din("xo", [NT, D])
    hflag_d = din("hflag", [128, 1])
    rc_d = din("rc", [128, 64])
    c_d = din("c", [1, D])
    pos_d = din("pos", [1, TOK], I32)
    invf_d = din("invf", [64, 1])
    w_ada = din("w_ada", [D, 3 * D])
    b_ada = din("b_ada", [1, 3 * D])
    w_in = din("w_in", [D, INW])
    b_gates = din("b_gates", [1, 2 * D])
    qng = din("q_norm_g", [1, 512])
    kvng = din("kv_norm_g", [1, 512])
    wq_d = din("wq", [512, 384])
    wkv_d = din("wkv", [512, 512])
    w_mla_o = din("w_mla_o", [D, D])
    w_pool_g = din("w_pool_g", [4, 256, 256])
    pool_scale = din("pool_scale", [1, 1024])
    w_pool_o = din("w_pool_o", [1024, D])
    w_out = din("w_out", [D, D])
    ln_g = din("ln_g", [1, D])
    ln_b = din("ln_b", [1, D])
    ident_d = din("ident", [128, 128], BF16)
    identf_d = din("identf", [128, 128])
    tri_d = din("tri", [128, 128], BF16)
    out_d = nc.dram_tensor("out", [TOK, D], F32, kind="ExternalOutput")

    lat_in = nc.dram_tensor("lat_in", [LATR, TOK], BF16)
    lat_out = nc.dram_tensor("lat_out", [NCORES * LATR, TOK], BF16)
    cs_in = nc.dram_tensor("cs_in", [128, TOK], F32)
    cs_out = nc.dram_tensor("cs_out", [NCORES * 128, TOK], F32)
    sg_d = nc.dram_tensor("sg_d", [128, 16 * TOK], BF16)
    gm_d = nc.dram_tensor("gm_d", [128, 16 * TOK], BF16)
    zp_d = nc.dram_tensor("zp_d", [128, 16 * TOK], BF16)
    g1_d = nc.dram_tensor("g1_d", [1, D], F32)
    yb_in = [nc.dram_tensor(f"yb_in{g}", [256, GT], F32) for g in range(NGRP)]
    yb_out = nc.dram_tensor("yb_out", [NGRP * D, GT], F32)
    ymine = nc.dram_tensor("ymine", [2 * D, GT], F32)

    dbg = {}
    if debug:
        dbg["lat"] = nc.dram_tensor("dbg_lat", [LATR, TOK], BF16, kind="ExternalOutput")
        dbg["cs"] = nc.dram_tensor("dbg_cs", [128, TOK], F32, kind="ExternalOutput")
        dbg["sg"] = nc.dram_tensor("dbg_sg", [128, 16 * TOK], BF16, kind="ExternalOutput")
        dbg["gm"] = nc.dram_tensor("dbg_gm", [128, 16 * TOK], BF16, kind="ExternalOutput")
        dbg["zp"] = nc.dram_tensor("dbg_zp", [128, 16 * TOK], BF16, kind="ExternalOutput")
        dbg["y"] = nc.dram_tensor("dbg_y", [NGRP * 256, GT], F32, kind="ExternalOutput")

    es = ExitStack()
    with es:
        ARENA_BYTES = 212480
        arena_t = es.enter_context(nc.sbuf_tensor("arena", [128, ARENA_BYTES // 4], F32))
        psum_t = es.enter_context(nc.psum_tensor("psum", [128, 8 * 512], F32))
        csem = {e: es.enter_context(nc.semaphore("cs_" + e)) for e in ("pe", "act", "dve", "pool")}
        asems = [es.enter_context(nc.semaphore(f"as{i}")) for i in range(N_ASEM)]
        ccsem = es.enter_context(nc.semaphore("ccs"))
        A = Arena(arena_t[:, :], ARENA_BYTES)
        ps = [psum_t[:, b * 512:(b + 1) * 512] for b in range(8)]
        R = Rec()
        PID = [None]

        def PS(b):
            return ("ps", b)

        def dma(eng, out, in_, r=(), w=()):
            R.add(eng, lambda e: e.dma_start(out=out, in_=in_), r, w, kind="d")

        def mm(out, lhsT, rhs, start, stop, r, w):
            R.add("pe", lambda e: e.matmul(out, lhsT, rhs, start=start, stop=stop), r, w)

        def act(out, in_, func, r, w, bias=None, scale=None, accum=None):
            kw = {}
            if bias is not None:
                kw["bias"] = bias
            if scale is not None:
                kw["scale"] = scale
            if accum is not None:
                kw["accum_out"] = accum
            R.add("act", lambda e: e.activation(out, in_, func, **kw), r, w)

        def ts2(eng, out, in0, s1, s2, op0, op1, r, w):
            if op1 is None:
                R.add(eng, lambda e: e.tensor_scalar(out, in0, s1, None, op0), r, w)
            else:
                R.add(eng, lambda e: e.tensor_scalar(out, in0, s1, s2, op0, op1), r, w)

        def tt(eng, out, in0, in1, op, r, w):
            R.add(eng, lambda e: e.tensor_tensor(out, in0, in1, op), r, w)

        def stt(eng, out, in0, scalar, in1, op0, op1, r, w):
            R.add(eng, lambda e: e.scalar_tensor_tensor(out, in0, scalar, in1, op0, op1), r, w)

        def cp(eng, out, in_, r, w):
            R.add(eng, lambda e: e.tensor_copy(out, in_), r, w)

        def memset(eng, ap, val, w):
            R.add(eng, lambda e: e.memset(ap, val), (), w)

        ident = A.alloc(256, BF16)
        identf = A.alloc(512, F32)
        tri = A.alloc(256, BF16)
        onesf = A.alloc(512, F32)
        colv = A.alloc(256, F32)
        scb = A.alloc(32, BF16)
        shiftc = A.alloc(64, F32)
        scale1c = A.alloc(64, F32)
        epsc = A.alloc(16, F32)
        hflag = A.alloc(4, F32)
        rc = A.alloc(256, F32)
        invf = A.alloc(4, F32)
        small = A.alloc(64, F32)
        dma("sp", ident, ident_d[:, :], w=["ident"])
        dma("sp", identf, identf_d[:, :], w=["identf"])
        dma("sp", tri, tri_d[:, :], w=["tri"])
        dma("sp", hflag, hflag_d[:, :], w=["hflag"])
        dma("sp", rc, rc_d[:, :], w=["rc"])
        dma("sp", invf[0:64, :], invf_d[:, :], w=["invf"])
        memset("dve", onesf, 1.0, ["onesf"])
        memset("dve", epsc[:, 0:1], 1e-5, ["epsc"])
        memset("dve", epsc[:, 1:2], 1e-6, ["epsc"])
        memset("dve", epsc[:, 2:3], -math.pi, ["epsc"])
        memset("dve", epsc[:, 3:4], 0.0, ["epsc"])

        P_TOP = A.mark()
        hT = A.alloc(16 * NT * 2, BF16).rearrange("p (k t) -> p k t", k=16)
        wbuf = [A.alloc(16 * 512 * 2, BF16).rearrange("p (k n) -> p k n", k=16) for _ in range(2)]
        wb_i = [0]
        M1 = A.mark()

        def load_w(pieces):
            s = wb_i[0] % 2
            wb_i[0] += 1
            for (src, c0, n, dc) in pieces:
                dma("pool", wbuf[s][:, :, dc:dc + n],
                    src[:, c0:c0 + n].rearrange("(k p) n -> p k n", p=128), w=[("wb", s)])
            return s

        sv = A.alloc(512, F32)
        brow = A.alloc(6144 * 4, F32)
        modrow = A.alloc(6144 * 4, F32)
        dma("sp", sv[0:16, :], c_d[:, :].rearrange("o (k p) -> (o k) p", p=128), w=["sv"])
        dma("sp", sv[16:20, :], qng[:, :].rearrange("o (k p) -> (o k) p", p=128), w=["sv"])
        dma("sp", sv[20:24, :], kvng[:, :].rearrange("o (k p) -> (o k) p", p=128), w=["sv"])
        dma("sp", sv[24:32, :], pool_scale[:, :].rearrange("o (k p) -> (o k) p", p=128), w=["sv"])
        dma("sp", sv[32:64, :], b_gates[:, :].rearrange("o (k p) -> (o k) p", p=128), w=["sv"])
        dma("sp", brow[0:1, :], b_ada[:, :], w=["brow"])
        act(sv[0:16, :], sv[0:16, :], AF.Silu, r=["sv"], w=["sv"])
        mm(ps[0][:, 0:64], sv[0:64, :], identf[0:64, 0:64], True, True, r=["sv", "identf"], w=[PS(0)])
        cp("dve", colv, ps[0][:, 0:64], r=[PS(0)], w=["colv"])
        cp("dve", scb, colv[:, 0:16], r=["colv"], w=["scb"])
        qgc, kvgc, pscc, bgc = colv[:, 16:20], colv[:, 20:24], colv[:, 24:32], colv[:, 32:64]
        for n in range(12):
            s = load_w([(w_ada, n * 512, 512, 0)])
            b = 1 + n % 2
            for k in range(16):
                mm(ps[b][0:1, :], scb[:, k:k + 1], wbuf[s][:, k, :], k == 0, k == 15,
                   r=["scb", ("wb", s)], w=[PS(b)])
            tt("dve", modrow[0:1, n * 512:(n + 1) * 512], ps[b][0:1, :], brow[0:1, n * 512:(n + 1) * 512],
               ALU.add, r=[PS(b), "brow"], w=["modrow"])
        for i in range(32):
            mm(ps[3][:, i:i + 1], modrow[0:1, i * 128:(i + 1) * 128], identf[0:1, 0:1], True, True,
               r=["modrow", "identf"], w=[PS(3)])
        cp("dve", shiftc, ps[3][:, 0:16], r=[PS(3)], w=["shiftc"])
        ts2("dve", scale1c, ps[3][:, 16:32], 1.0, None, ALU.add, None, r=[PS(3)], w=["scale1c"])
        ts2("dve", modrow[0:1, 4096:6144], modrow[0:1, 4096:6144], 1.0, None, ALU.add, None,
            r=["modrow"], w=["modrow"])
        dma("sp", g1_d[:, :], modrow[0:1, 4096:6144], r=["modrow"], w=["g1_d"])
        R.barrier()
        A.release(M1)

        xt = [A.alloc(D * 4, F32) for _ in range(2)]
        xn = [A.alloc(D * 2, BF16) for _ in range(2)]
        junk = A.alloc(D * 4, F32)
        for t in range(NT // 128):
            s = t % 2
            sm = small[:, s * 8:(s + 1) * 8]
            SM = ("sm", s)
            dma("sp", xt[s], xo[t * 128:(t + 1) * 128, :], w=[("xt", s)])
            memset("dve", sm[:, 0:3], 0.0, [SM])
            act(junk, xt[s], AF.Identity, r=[("xt", s), SM], w=["junk", SM], accum=sm[:, 0:1])
            ts2("dve", sm[:, 1:2], sm[:, 0:1], -1.0 / D, None, ALU.mult, None, r=[SM], w=[SM])
            act(junk, xt[s], AF.Square, r=[("xt", s), SM], w=["junk", SM], bias=sm[:, 1:2], accum=sm[:, 2:3])
            act(sm[:, 3:4], sm[:, 2:3], AF.Sqrt, r=[SM, "epsc"], w=[SM], bias=epsc[:, 0:1], scale=1.0 / D)
            R.add("dve", lambda e, o=sm[:, 4:5], i=sm[:, 3:4]: e.reciprocal(o, i), [SM], [SM])
            tt("dve", sm[:, 5:6], sm[:, 1:2], sm[:, 4:5], ALU.mult, r=[SM], w=[SM])
            act(xn[s], xt[s], AF.Identity, r=[("xt", s), SM], w=[("xn", s)], bias=sm[:, 5:6], scale=sm[:, 4:5])
            b0 = 2 * (t % 2)
            for k in range(16):
                pb = ps[b0 + k // 8].bitcast(BF16)
                R.add("pe", lambda e, o=pb[:, (k % 8) * 128:(k % 8 + 1) * 128], i=xn[s][:, k * 128:(k + 1) * 128]:
                      e.transpose(o, i, ident), [("xn", s), "ident"], [PS(b0 + k // 8)])
            for k in range(16):
                pb = ps[b0 + k // 8].bitcast(BF16)
                ts2("dve", hT[:, k, t * 128:(t + 1) * 128], pb[:, (k % 8) * 128:(k % 8 + 1) * 128],
                    scale1c[:, k:k + 1], shiftc[:, k:k + 1], ALU.mult, ALU.add,
                    r=[PS(b0 + k // 8), "scale1c", "shiftc"], w=[("hT", t)])
            if t == 0:
                ts2("dve", hT[:, :, 0:128], hT[:, :, 0:128], hflag[:, 0:1], None, ALU.mult, None,
                    r=[("hT", 0), "hflag"], w=[("hT", 0)])
        R.barrier()
        A.release(M1)
        HT_ALL = [("hT", t) for t in range(NT // 128)]

        bank_rr = [0]

        def nbank():
            b = bank_rr[0] % 8
            bank_rr[0] += 1
            return b

        def proj(lhs_fn, M, c0, n, consumer, nk=16, rhs_fn=None, rkeys=None):
            b = nbank()
            for k in range(nk):
                rhs = hT[:, k, c0:c0 + n] if rhs_fn is None else rhs_fn(k)
                mm(ps[b][0:M, 0:n], lhs_fn(k), rhs, k == 0, k == nk - 1,
                   r=(rkeys if rkeys is not None else HT_ALL + [("wb", 0), ("wb", 1)]), w=[PS(b)])
            consumer(b)

        CS = A.alloc(2 * TOK * 4, F32).rearrange("p (c t) -> p c t", c=2)
        posi = A.alloc(TOK * 4, I32)
        ang = A.alloc(TOK * 4, F32)
        rr = A.alloc(TOK * 4, F32)
        dma("sp", posi[0:64, :], pos_d[:, :].partition_broadcast(64), w=["posi"])
        cp("dve", ang[0:64, :], posi[0:64, :], r=["posi"], w=["ang"])
        ts2("dve", ang[0:64, :], ang[0:64, :], invf[0:64, 0:1], None, ALU.mult, None, r=["ang", "invf"], w=["ang"])
        C1 = 6.28125
        C2 = 2 * math.pi - C1
        qi = posi

        def reduce_angle(src_add):
            ts2("dve", rr[0:64, :], ang[0:64, :], src_add, 1.0 / (2 * math.pi), ALU.add, ALU.mult, r=["ang", "CS"], w=["rr"])
            cp("dve", qi[0:64, :], rr[0:64, :], r=["rr"], w=["posi"])
            cp("dve", rr[0:64, :], qi[0:64, :], r=["posi"], w=["rr"])
            stt("dve", t2r[0:64, :], rr[0:64, :], -C1, ang[0:64, :], ALU.mult, ALU.add, r=["rr", "ang"], w=["t2r"])
            stt("dve", t2r[0:64, :], rr[0:64, :], -C2, t2r[0:64, :], ALU.mult, ALU.add, r=["rr", "t2r"], w=["t2r"])
            if src_add != 0.0:
                ts2("dve", t2r[0:64, :], t2r[0:64, :], src_add, None, ALU.add, None, r=["t2r"], w=["t2r"])
            ts2("dve", rr[0:64, :], t2r[0:64, :], math.pi, 2 * math.pi, ALU.is_gt, ALU.mult, r=["t2r"], w=["rr"])
            tt("dve", t2r[0:64, :], t2r[0:64, :], rr[0:64, :], ALU.subtract, r=["t2r", "rr"], w=["t2r"])
            ts2("dve", rr[0:64, :], t2r[0:64, :], -math.pi, 2 * math.pi, ALU.is_lt, ALU.mult, r=["t2r"], w=["rr"])
            tt("dve", rr[0:64, :], t2r[0:64, :], rr[0:64, :], ALU.add, r=["t2r", "rr"], w=["rr"])

        t2r = A.alloc(TOK * 4, F32)
        reduce_angle(0.0)
        act(CS[0:64, 1, :], rr[0:64, :], AF.Sin, r=["rr"], w=["CS"])
        ts2("dve", CS[0:32, 1, :], CS[0:32, 1, :], -1.0, None, ALU.mult, None, r=["CS"], w=["CS"])
        reduce_angle(math.pi / 2)
        act(CS[0:64, 0, :], rr[0:64, :], AF.Sin, r=["rr"], w=["CS"])
        dma("sp", cs_in[0:64, :], CS[0:64, 0, :], r=["CS"], w=["cs_in"])
        dma("sp", cs_in[64:128, :], CS[0:64, 1, :], r=["CS"], w=["cs_in"])

        ql = A.alloc(4 * TOK * 4, F32).rearrange("p (k t) -> p k t", k=4)
        sq = [A.alloc(512 * 4, F32) for _ in range(2)]
        rstd = A.alloc(512 * 4, F32)
        stg = [A.alloc(TOK * 2, BF16) for _ in range(2)]
        stg_i = [0]
        t1 = A.alloc(512 * 4, F32)
        t2 = A.alloc(512 * 4, F32)

        for li, (c0w, gcol, row0) in enumerate(((C_QLAT, qgc, 0), (C_CKV, kvgc, 512))):
            s = load_w([(w_in, c0w, 512, 0)])
            SSQ = 6 + li
            for tt_ in range(4):
                c0 = HALO + tt_ * 512
                for m in range(4):
                    def cons(b, m=m, tt_=tt_):
                        act(ql[:, m, tt_ * 512:(tt_ + 1) * 512], ps[b], AF.Copy, r=[PS(b)], w=[("ql", tt_)])
                        act(sq[m % 2], ps[b], AF.Square, r=[PS(b)], w=[("sq", m % 2)])
                        mm(ps[SSQ], onesf, sq[m % 2], m == 0, m == 3, r=[("sq", m % 2), "onesf"], w=[PS(SSQ)])
                    b = bank_rr[0] % 6
                    bank_rr[0] += 1
                    for k in range(16):
                        mm(ps[b][:, :], wbuf[s][:, k, m * 128:(m + 1) * 128], hT[:, k, c0:c0 + 512],
                           k == 0, k == 15, r=HT_ALL + [("wb", s)], w=[PS(b)])
                    cons(b)
                act(rstd, ps[SSQ], AF.Sqrt, r=[PS(SSQ), "epsc"], w=["rstd"], bias=epsc[:, 1:2], scale=1.0 / 512)
                R.add("dve", lambda e: e.reciprocal(rstd, rstd), ["rstd"], ["rstd"])
                for m in range(4):
                    si = stg_i[0] % 2
                    stg_i[0] += 1
                    stt("dve", stg[si][:, 0:512], ql[:, m, tt_ * 512:(tt_ + 1) * 512], gcol[:, m:m + 1], rstd,
                        ALU.mult, ALU.mult, r=[("ql", tt_), "rstd", "colv"], w=[("stg", si)])
                    dma("sp", lat_in[row0 + m * 128:row0 + (m + 1) * 128, tt_ * 512:(tt_ + 1) * 512],
                        stg[si][:, 0:512], r=[("stg", si)], w=[("lat_in", row0 + m * 128, tt_)])
        s = load_w([(w_in, C_KPE, 64, 0), (w_in, C_KPE + 32, 32, 64), (w_in, C_KPE, 32, 96)])
        for tt_ in range(4):
            c0 = HALO + tt_ * 512
            bA = bank_rr[0] % 6
            bB = (bank_rr[0] + 1) % 6
            bank_rr[0] += 2
            for k in range(16):
                mm(ps[bA][0:64, :], wbuf[s][:, k, 0:64], hT[:, k, c0:c0 + 512], k == 0, k == 15,
                   r=HT_ALL + [("wb", s)], w=[PS(bA)])
            for k in range(16):
                mm(ps[bB][0:64, :], wbuf[s][:, k, 64:128], hT[:, k, c0:c0 + 512], k == 0, k == 15,
                   r=HT_ALL + [("wb", s)], w=[PS(bB)])
            tt("dve", t1[0:64, :], ps[bA][0:64, :], CS[0:64, 0, tt_ * 512:(tt_ + 1) * 512], ALU.mult,
               r=[PS(bA), "CS"], w=["t1"])
            tt("dve", t2[0:64, :], ps[bB][0:64, :], CS[0:64, 1, tt_ * 512:(tt_ + 1) * 512], ALU.mult,
               r=[PS(bB), "CS"], w=["t2"])
            si = stg_i[0] % 2
            stg_i[0] += 1
            tt("dve", stg[si][0:64, 0:512], t1[0:64, :], t2[0:64, :], ALU.add, r=["t1", "t2"], w=[("stg", si)])
            dma("sp", lat_in[1024:1088, tt_ * 512:(tt_ + 1) * 512], stg[si][0:64, 0:512],
                r=[("stg", si)], w=[("lat_in", 1024, tt_)])
        R.add("pool", lambda e: e.collective_compute("AllGather", ALU.bypass, replica_groups=[list(range(NCORES))],
                                                     ins=[lat_in.ap().opt()], outs=[lat_out.ap().opt()]),
              [("lat_in", r0, t_) for r0 in range(0, 1088, 128) for t_ in range(4)], ["lat_out"], kind="cc")
        R.add("pool", lambda e: e.collective_compute("AllGather", ALU.bypass, replica_groups=[list(range(NCORES))],
                                                     ins=[cs_in.ap().opt()], outs=[cs_out.ap().opt()]),
              ["cs_in"], ["cs_out"], kind="cc")
        if debug:
            dma("sp", dbg["lat"][:, :], lat_in[:, :],
                r=[("lat_in", r0, t_) for r0 in range(0, 1088, 128) for t_ in range(4)], w=["dbg_lat"])
            dma("sp", dbg["cs"][:, :], cs_in[:, :], r=["cs_in"], w=["dbg_cs"])
        R.barrier()
        A.release(M1)

        stg = [A.alloc(TOK * 2, BF16) for _ in range(2)]
        for (cbase, func, dst, dname, bias_c0) in ((C_MG, AF.Silu, sg_d, "sg_d", None),
                                                   (C_GL, AF.Sigmoid, gm_d, "gm_d", 0)):
            for grp in range(4):
                s = load_w([(w_in, cbase + grp * 512, 512, 0)])
                for m4 in range(4):
                    m = grp * 4 + m4
                    si = stg_i[0] % 2
                    stg_i[0] += 1
                    for tt_ in range(4):
                        def cons(b, tt_=tt_, si=si, m=m):
                            if bias_c0 is None:
                                act(stg[si][:, tt_ * 512:(tt_ + 1) * 512], ps[b], func, r=[PS(b)], w=[("stg", si)])
                            else:
                                act(stg[si][:, tt_ * 512:(tt_ + 1) * 512], ps[b], func, r=[PS(b), "colv"],
                                    w=[("stg", si)], bias=bgc[:, bias_c0 + m:bias_c0 + m + 1])
                        proj(lambda k, s=s, m4=m4: wbuf[s][:, k, m4 * 128:(m4 + 1) * 128], 128,
                             HALO + tt_ * 512, 512, cons, rkeys=HT_ALL + [("wb", s)])
                    dma("sp", dst[:, m * TOK:(m + 1) * TOK], stg[si], r=[("stg", si)], w=[(dname, m)])

        wpg = A.alloc(8 * 256 * 2, BF16).rearrange("p (j n) -> p j n", j=8)
        for g in range(4):
            for ic in range(2):
                dma("pool", wpg[:, g * 2 + ic, :], w_pool_g[g, ic * 128:(ic + 1) * 128, :], w=["wpg"])
        u = A.alloc(NT * 4, F32)
        sA = A.alloc(NT * 4, F32)
        sB = A.alloc(NT * 4, F32)
        pl = A.alloc(2 * TOK * 2, BF16).rearrange("p (c t) -> p c t", c=2)
        sgp = A.alloc(2 * TOK * 2, BF16).rearrange("p (c t) -> p c t", c=2)
        pm = A.alloc(8 * TOK * 2, BF16).rearrange("p (k t) -> p k t", k=8)
        tmp16 = A.alloc(64, F32)
        for g in range(4):
            wdw = 2 ** (g + 1)
            s = load_w([(w_in, C_PI + g * 256, 256, 0), (w_in, C_PG + g * 256, 256, 256)])
            for c in range(2):
                def cons_h(b):
                    act(u[:, 0:128], ps[b][:, 0:128], AF.Copy, r=[PS(b)], w=["u"])
                proj(lambda k, s=s, c=c: wbuf[s][:, k, c * 128:(c + 1) * 128], 128, 0, 128, cons_h,
                     rkeys=HT_ALL + [("wb", s)])
                for tt_ in range(4):
                    def cons(b, tt_=tt_):
                        act(u[:, HALO + tt_ * 512:HALO + (tt_ + 1) * 512], ps[b], AF.Copy, r=[PS(b)], w=["u"])
                    proj(lambda k, s=s, c=c: wbuf[s][:, k, c * 128:(c + 1) * 128], 128, HALO + tt_ * 512, 512,
                         cons, rkeys=HT_ALL + [("wb", s)])
                tt("dve", sA[:, 1:NT], u[:, 1:NT], u[:, 0:NT - 1], ALU.add, r=["u"], w=["sA"])
                cur, oth, ck, ok_ = sA, sB, "sA", "sB"
                sh = 2
                while sh < wdw:
                    lo = 2 * sh - 1
                    tt("dve", oth[:, lo:NT], cur[:, lo:NT], cur[:, lo - sh:NT - sh], ALU.add, r=[ck], w=[ok_])
                    cur, oth, ck, ok_ = oth, cur, ok_, ck
                    sh *= 2
                stt("dve", pl[:, c, :], cur[:, HALO:NT], 1.0 / wdw, u[:, HALO:NT], ALU.mult, ALU.subtract,
                    r=[ck, "u"], w=["pl"])
                tt("dve", tmp16[:, 0:16], cur[:, HALO:HALO + 16], rc[:, g * 16:(g + 1) * 16], ALU.mult,
                   r=[ck, "rc"], w=["tmp16"])
                tt("dve", pl[:, c, 0:16], tmp16[:, 0:16], u[:, HALO:HALO + 16], ALU.subtract,
                   r=["tmp16", "u", "pl"], w=["pl"])
                for tt_ in range(4):
                    def cons(b, tt_=tt_, c=c):
                        act(sgp[:, c, tt_ * 512:(tt_ + 1) * 512], ps[b], AF.Silu, r=[PS(b)], w=["sgp"])
                    proj(lambda k, s=s, c=c: wbuf[s][:, k, 256 + c * 128:256 + (c + 1) * 128], 128,
                         HALO + tt_ * 512, 512, cons, rkeys=HT_ALL + [("wb", s)])
            for oc in range(2):
                for tt_ in range(4):
                    def cons(b, tt_=tt_, oc=oc, g=g):
                        stt("dve", pm[:, g * 2 + oc, tt_ * 512:(tt_ + 1) * 512], ps[b],
                            pscc[:, g * 2 + oc:g * 2 + oc + 1], sgp[:, oc, tt_ * 512:(tt_ + 1) * 512],
                            ALU.mult, ALU.mult, r=[PS(b), "sgp", "colv"], w=["pm"])
                    proj(lambda ic, g=g, oc=oc: wpg[:, g * 2 + ic, oc * 128:(oc + 1) * 128], 128, 0, 512, cons,
                         nk=2, rhs_fn=lambda ic, tt_=tt_: pl[:, ic, tt_ * 512:(tt_ + 1) * 512],
                         rkeys=["wpg", "pl"])
        wpo = [A.alloc(8 * 512 * 2, BF16).rearrange("p (k n) -> p k n", k=8) for _ in range(1)]
        gp = [A.alloc(512 * 4, F32) for _ in range(2)]
        gp_i = 0
        for grp in range(4):
            s = load_w([(w_in, C_GL + D + grp * 512, 512, 0)])
            ws = 0
            dma("pool", wpo[ws], w_pool_o[:, grp * 512:(grp + 1) * 512].rearrange("(k p) n -> p k n", p=128),
                w=[("wpo", ws)])
            for m4 in range(4):
                m = grp * 4 + m4
                si = stg_i[0] % 2
                stg_i[0] += 1
                for tt_ in range(4):
                    gi = gp_i % 2
                    gp_i += 1

                    def cons_g(b, gi=gi, m=m):
                        act(gp[gi], ps[b], AF.Sigmoid, r=[PS(b), "colv"], w=[("gp", gi)],
                            bias=bgc[:, 16 + m:16 + m + 1])
                    proj(lambda k, s=s, m4=m4: wbuf[s][:, k, m4 * 128:(m4 + 1) * 128], 128,
                         HALO + tt_ * 512, 512, cons_g, rkeys=HT_ALL + [("wb", s)])

                    def cons_z(b, gi=gi, si=si, tt_=tt_):
                        tt("dve", stg[si][:, tt_ * 512:(tt_ + 1) * 512], ps[b], gp[gi], ALU.mult,
                           r=[PS(b), ("gp", gi)], w=[("stg", si)])
                    proj(lambda k8, ws=ws, m4=m4: wpo[ws][:, k8, m4 * 128:(m4 + 1) * 128], 128, 0, 512, cons_z,
                         nk=8, rhs_fn=lambda k8, tt_=tt_: pm[:, k8, tt_ * 512:(tt_ + 1) * 512],
                         rkeys=["pm", ("wpo", ws)])
                dma("sp", zp_d[:, m * TOK:(m + 1) * TOK], stg[si], r=[("stg", si)], w=[("zp_d", m)])
        R.barrier()
        A.release(P_TOP)
        if debug:
            dma("sp", dbg["sg"][:, :], sg_d[:, :], w=["dbg_sg"])
            dma("sp", dbg["gm"][:, :], gm_d[:, :], w=["dbg_gm"])
            dma("sp", dbg["zp"][:, :], zp_d[:, :], w=["dbg_zp"])

        KT = A.alloc(2 * S * 2, BF16).rearrange("p (a t) -> p a t", a=2)
        V = A.alloc(128 * 256 * 2, BF16).rearrange("p (b n) -> p b n", b=128)
        KPE = A.alloc(S * 2, BF16)
        wq = A.alloc(4 * 512 * 2, BF16).rearrange("p (k n) -> p k n", k=4)
        wkv = A.alloc(4 * 512 * 2, BF16).rearrange("p (k n) -> p k n", k=4)
        qlat = [A.alloc(4 * 512 * 2, BF16).rearrange("p (k n) -> p k n", k=4) for _ in range(1)]
        ckv = [A.alloc(4 * 512 * 2, BF16).rearrange("p (k n) -> p k n", k=4) for _ in range(1)]
        cst = [A.alloc(2 * 512 * 4, F32).rearrange("p (c n) -> p c n", c=2) for _ in range(1)]
        QN = A.alloc(2 * 512 * 2, BF16).rearrange("p (a n) -> p a n", a=2)
        QR = A.alloc(512 * 2, BF16)
        NPT = 3
        pT2 = [A.alloc(2 * 512 * 2, BF16).rearrange("p (a n) -> p a n", a=2) for _ in range(NPT)]
        acc = A.alloc(2 * 512 * 4, F32).rearrange("p (a n) -> p a n", a=2)
        rcp = A.alloc(512 * 4, F32)
        yT = [A.alloc(512 * 4, F32) for _ in range(2)]

        def kmaj(src_rows):
            return src_rows.rearrange("(k p) n -> p k n", p=128)

        for (c0, n, dc) in ((0, 128, 0), (192, 128, 128), (128, 64, 256), (320, 64, 320),
                            (160, 32, 384), (128, 32, 416), (352, 32, 448), (320, 32, 480)):
            dma("pool", wq[:, :, dc:dc + n], kmaj(wq_d[:, c0:c0 + n]), w=["wq"])
        for (c0, n, dc) in ((0, 128, 0), (256, 128, 128), (128, 128, 256), (384, 128, 384)):
            dma("pool", wkv[:, :, dc:dc + n], kmaj(wkv_d[:, c0:c0 + n]), w=["wkv"])

        step_i = [0]
        yt_i = [0]

        def load_tile(j):
            s = 0
            rk, t0 = j // 4, (j % 4) * 512
            base = rk * LATR
            dma("sp", qlat[s], kmaj(lat_out[base:base + 512, t0:t0 + 512]), r=["lat_out"], w=[("qlat", s)])
            dma("sp", ckv[s], kmaj(lat_out[base + 512:base + 1024, t0:t0 + 512]), r=["lat_out"], w=[("ckv", s)])
            for h in range(2):
                dma("sp", KPE[h * 64:(h + 1) * 64, j * 512:(j + 1) * 512],
                    lat_out[base + 1024:base + 1088, t0:t0 + 512], r=["lat_out"], w=[("KPE", j)])
                for c in range(2):
                    dma("sp", cst[s][h * 64:(h + 1) * 64, c, :],
                        cs_out[rk * 128 + c * 64:rk * 128 + (c + 1) * 64, t0:t0 + 512],
                        r=["cs_out"], w=[("cst", s)])

        load_tile(0)
        for j in range(NQT):
            s = 0
            for a in range(2):
                for k in range(4):
                    mm(ps[6 + a], wkv[:, k, a * 128:(a + 1) * 128], ckv[s][:, k, :], k == 0, k == 3,
                       r=["wkv", ("ckv", s)], w=[PS(6 + a)])
                act(KT[:, a, j * 512:(j + 1) * 512], ps[6 + a], AF.Copy, r=[PS(6 + a)], w=[("KT", j)])
            for half in range(2):
                for tb2 in range(2):
                    tb = half * 2 + tb2
                    for k in range(4):
                        mm(ps[7][:, tb2 * 256:(tb2 + 1) * 256], ckv[s][:, k, tb * 128:(tb + 1) * 128],
                           wkv[:, k, 256:512], k == 0, k == 3, r=["wkv", ("ckv", s)], w=[PS(7)])
                cp("dve", V[:, 4 * j + half * 2:4 * j + half * 2 + 2, :],
                   ps[7].rearrange("p (b n) -> p b n", b=2), r=[PS(7)], w=[("V", j)])
            for a in range(2):
                for k in range(4):
                    mm(ps[6 + a], wq[:, k, a * 128:(a + 1) * 128], qlat[s][:, k, :], k == 0, k == 3,
                       r=["wq", ("qlat", s)], w=[PS(6 + a)])
                act(QN[:, a, :], ps[6 + a], AF.Copy, r=[PS(6 + a)], w=["QN"])
            for v_ in range(2):
                for k in range(4):
                    mm(ps[6 + v_], wq[:, k, 256 + v_ * 128:256 + (v_ + 1) * 128], qlat[s][:, k, :], k == 0, k == 3,
                       r=["wq", ("qlat", s)], w=[PS(6 + v_)])
            tt("dve", yT[0], ps[6], cst[s][:, 0, :], ALU.mult, r=[PS(6), ("cst", s)], w=[("yT", 0)])
            tt("dve", rcp, ps[7], cst[s][:, 1, :], ALU.mult, r=[PS(7), ("cst", s)], w=["rcp"])
            tt("dve", QR, yT[0], rcp, ALU.add, r=[("yT", 0), "rcp"], w=["QR"])
            if j + 1 < NQT:
                load_tile(j + 1)

            nkb = 4 * j + 4
            info = {}

            def issue_S(kb):
                q0 = max(0, kb - 4 * j) * 128
                g_i = step_i[0]
                step_i[0] += 1
                sbp, slot = (g_i % 2) * 2, g_i % NPT
                info[kb] = (q0, slot)
                for a in range(2):
                    mm(ps[sbp + a][:, q0:512], KT[:, a, kb * 128:(kb + 1) * 128], QN[:, a, q0:512], True, False,
                       r=[("KT", kb // 4), "QN"], w=[PS(sbp + a)])
                for a in range(2):
                    mm(ps[sbp + a][:, q0:512], KPE[a * 64:(a + 1) * 64, kb * 128:(kb + 1) * 128],
                       QR[a * 64:(a + 1) * 64, q0:512], False, True, r=[("KPE", kb // 4), "QR"], w=[PS(sbp + a)])
                for a in range(2):
                    act(pT2[slot][:, a, q0:512], ps[sbp + a][:, q0:512], AF.Exp, r=[PS(sbp + a)],
                        w=[("pT", slot, a)], scale=SCALE)
                if kb >= 4 * j:
                    for a in range(2):
                        tt("dve", pT2[slot][:, a, q0:q0 + 128], pT2[slot][:, a, q0:q0 + 128], tri, ALU.mult,
                           r=[("pT", slot, a), "tri"], w=[("pT", slot, a)])
                if kb == 0:
                    cp("dve", acc[:, :, :], pT2[slot][:, :, :], r=[("pT", slot, 0), ("pT", slot, 1)], w=["acc"])
                else:
                    tt("dve", acc[:, :, q0:512], acc[:, :, q0:512], pT2[slot][:, :, q0:512], ALU.add,
                       r=[("pT", slot, 0), ("pT", slot, 1), "acc"], w=["acc"])

            def issue_PV(kb):
                q0, slot = info[kb]
                for a in range(2):
                    mm(ps[4 + a][:, q0:512], V[:, kb, a * 128:(a + 1) * 128], pT2[slot][:, a, q0:512],
                       kb == 0, kb == nkb - 1, r=[("V", kb // 4), ("pT", slot, a)], w=[PS(4 + a)])

            LOOK = 1
            for idx in range(nkb + LOOK):
                if idx < nkb:
                    issue_S(idx)
                if idx >= LOOK:
                    issue_PV(idx - LOOK)
            g = j // 2
            for a in range(2):
                mm(ps[6 + a], onesf, acc[:, a, :], True, True, r=["acc", "onesf"], w=[PS(6 + a)])
                R.add("dve", lambda e, i=ps[6 + a]: e.reciprocal(rcp, i), [PS(6 + a)], ["rcp"])
                yi = yt_i[0] % 2
                yt_i[0] += 1
                tt("dve", yT[yi], ps[4 + a], rcp, ALU.mult, r=[PS(4 + a), "rcp"], w=[("yT", yi)])
                dma("sp", yb_in[g][a * 128:(a + 1) * 128, (j % 2) * 512:(j % 2 + 1) * 512], yT[yi],
                    r=[("yT", yi)], w=[("yb_in", g)])
            if j % 2 == 1:
                R.add("pool", lambda e, g=g: e.collective_compute(
                    "AllGather", ALU.bypass, replica_groups=[list(range(NCORES))],
                    ins=[yb_in[g].ap().opt()], outs=[yb_out[g * D:(g + 1) * D, :].opt()]),
                    [("yb_in", g)], ["yb_out"], kind="cc")
                if debug:
                    dma("sp", dbg["y"][g * 256:(g + 1) * 256, :], yb_in[g][:, :], r=[("yb_in", g)], w=["dbg_y"])
        R.barrier()
        A.release(P_TOP)

        zall = A.alloc(16 * TOK * 2, BF16).rearrange("p (k t) -> p k t", k=16)
        M3 = A.mark()
        wmo = A.alloc(16 * D * 2, BF16).rearrange("p (k n) -> p k n", k=16)
        ym = A.alloc(16 * 512 * 2, BF16).rearrange("p (k n) -> p k n", k=16)
        yat = [A.alloc(512 * 4, F32) for _ in range(3)]
        sgt = [A.alloc(512 * 2, BF16) for _ in range(3)]
        gmt = [A.alloc(512 * 2, BF16) for _ in range(2)]
        zpt = [A.alloc(512 * 2, BF16) for _ in range(2)]
        zt = A.alloc(512 * 4, F32)
        for q4 in range(4):
            dma("pool", wmo[:, :, q4 * 512:(q4 + 1) * 512], kmaj(w_mla_o[:, q4 * 512:(q4 + 1) * 512]), w=["wmo"])
        R.add("sp", lambda e: e.dma_start(out=ymine[:, :], in_=yb_out[bass.ts(PID[0], 2 * D), :]),
              ["yb_out"], ["ymine"], kind="d")
        li = 0
        for tt_ in range(4):
            tcol = (tt_ % 2) * 512
            for k in range(16):
                s3 = li % 3
                li += 1
                r0 = (tt_ // 2) * D + k * 128
                dma("sp", yat[s3], ymine[r0:r0 + 128, tcol:tcol + 512], r=["ymine"], w=[("yat", s3)])
                dma("sp", sgt[s3], sg_d[:, k * TOK + tt_ * 512:k * TOK + (tt_ + 1) * 512], r=["sg_d"], w=[("sgt", s3)])
                tt("dve", ym[:, k, :], yat[s3], sgt[s3], ALU.mult, r=[("yat", s3), ("sgt", s3)], w=["ym"])
            for m in range(16):
                s2 = m % 2
                dma("sp", gmt[s2], gm_d[:, m * TOK + tt_ * 512:m * TOK + (tt_ + 1) * 512], r=["gm_d"], w=[("gmt", s2)])
                dma("sp", zpt[s2], zp_d[:, m * TOK + tt_ * 512:m * TOK + (tt_ + 1) * 512], r=["zp_d"], w=[("zpt", s2)])
                b = nbank()
                for k in range(16):
                    mm(ps[b], wmo[:, k, m * 128:(m + 1) * 128], ym[:, k, :], k == 0, k == 15, r=["wmo", "ym"], w=[PS(b)])
                tt("dve", zt, ps[b], gmt[s2], ALU.mult, r=[PS(b), ("gmt", s2)], w=["zt"])
                tt("dve", zall[:, m, tt_ * 512:(tt_ + 1) * 512], zt, zpt[s2], ALU.add, r=["zt", ("zpt", s2)], w=["zall"])
        R.barrier()
        A.release(M3)
        wout = A.alloc(16 * D * 2, BF16).rearrange("p (k n) -> p k n", k=16)
        g1b = A.alloc(D * 4, F32)
        lgb = A.alloc(D * 4, F32)
        lbb = A.alloc(D * 4, F32)
        xt = [A.alloc(D * 4, F32) for _ in range(2)]
        ub = A.alloc(D * 4, F32)
        ob = [A.alloc(D * 4, F32) for _ in range(2)]
        junk = A.alloc(D * 4, F32)
        for q4 in range(4):
            dma("pool", wout[:, :, q4 * 512:(q4 + 1) * 512], kmaj(w_out[:, q4 * 512:(q4 + 1) * 512]), w=["wout"])
        dma("sp", g1b, g1_d[:, :].partition_broadcast(128), r=["g1_d"], w=["g1b"])
        dma("sp", lgb, ln_g[:, :].partition_broadcast(128), w=["lgb"])
        dma("sp", lbb, ln_b[:, :].partition_broadcast(128), w=["lbb"])
        for tb in range(TOK // 128):
            s = tb % 2
            sm = small[:, s * 8:(s + 1) * 8]
            SM = ("sm", s)
            dma("sp", xt[s], xo[HALO + tb * 128:HALO + (tb + 1) * 128, :], w=[("xt", s)])
            hb = 4 * (tb % 2)
            for n4 in range(4):
                for k in range(16):
                    mm(ps[hb + n4], zall[:, k, tb * 128:(tb + 1) * 128], wout[:, k, n4 * 512:(n4 + 1) * 512],
                       k == 0, k == 15, r=["zall", "wout"], w=[PS(hb + n4)])
            for n4 in range(4):
                tt("dve", ub[:, n4 * 512:(n4 + 1) * 512], ps[hb + n4], g1b[:, n4 * 512:(n4 + 1) * 512], ALU.mult,
                   r=[PS(hb + n4), "g1b"], w=["ub"])
            stt("dve", ub, xt[s], ALPHA, ub, ALU.mult, ALU.add, r=[("xt", s), "ub"], w=["ub"])
            memset("dve", sm[:, 0:3], 0.0, [SM])
            act(junk, ub, AF.Identity, r=["ub", SM], w=["junk", SM], accum=sm[:, 0:1])
            ts2("dve", sm[:, 1:2], sm[:, 0:1], -1.0 / D, None, ALU.mult, None, r=[SM], w=[SM])
            act(junk, ub, AF.Square, r=["ub", SM], w=["junk", SM], bias=sm[:, 1:2], accum=sm[:, 2:3])
            act(sm[:, 3:4], sm[:, 2:3], AF.Sqrt, r=[SM, "epsc"], w=[SM], bias=epsc[:, 0:1], scale=1.0 / D)
            R.add("dve", lambda e, o=sm[:, 4:5], i=sm[:, 3:4]: e.reciprocal(o, i), [SM], [SM])
            tt("dve", sm[:, 5:6], sm[:, 1:2], sm[:, 4:5], ALU.mult, r=[SM], w=[SM])
            act(ob[s], ub, AF.Identity, r=["ub", SM], w=[("ob", s)], bias=sm[:, 5:6], scale=sm[:, 4:5])
            tt("pool", ob[s], ob[s], lgb, ALU.mult, r=[("ob", s), "lgb"], w=[("ob", s)])
            tt("pool", ob[s], ob[s], lbb, ALU.add, r=[("ob", s), "lbb"], w=[("ob", s)])
            dma("sp", out_d[tb * 128:(tb + 1) * 128, :], ob[s], r=[("ob", s)], w=["out"])
        R.barrier()

        R.finalize()
        with nc.Block() as block:
            @block.tensor
            def _(e):
                R.emit("pe", e, csem, asems, ccsem)

            @block.scalar
            def _(e):
                R.emit("act", e, csem, asems, ccsem)

            @block.vector
            def _(e):
                R.emit("dve", e, csem, asems, ccsem)

            @block.gpsimd
            def _(e):
                R.emit("pool", e, csem, asems, ccsem)

            @block.sync
            def _(e):
                PID[0] = nc.partition_id([mybir.EngineType.SP])
                R.emit("sp", e, csem, asems, ccsem)
    return nc


def make_in_maps(inputs):
    x = np.asarray(inputs["x"], np.float32)[0]
    pos = np.asarray(inputs["positions"], np.int32)
    g = lambda k: np.ascontiguousarray(np.asarray(inputs[k], np.float32)[0])
    wqb, wkvb = g("w_q_b"), g("w_kv_b")
    invf = (np.float32(10000.0) ** (-np.arange(0, 64, 2, dtype=np.float32) / np.float32(64))).astype(np.float32)
    invf = np.concatenate([invf, invf]).reshape(64, 1)
    ident = np.eye(128, dtype=np.float32)
    tri = (np.arange(128)[:, None] <= np.arange(128)[None, :]).astype(np.float32)
    common = dict(
        c=np.asarray(inputs["c"], np.float32), invf=invf, w_ada=g("w_ada"),
        b_ada=np.asarray(inputs["b_ada"], np.float32), w_in=g("w_in"),
        b_gates=np.asarray(inputs["b_gates"], np.float32), q_norm_g=np.asarray(inputs["q_norm_g"], np.float32),
        kv_norm_g=np.asarray(inputs["kv_norm_g"], np.float32), w_mla_o=g("w_mla_o"), w_pool_g=g("w_pool_g"),
        pool_scale=np.asarray(inputs["pool_scale"], np.float32), w_pool_o=g("w_pool_o"), w_out=g("w_out"),
        ln_g=np.asarray(inputs["ln_g"], np.float32), ln_b=np.asarray(inputs["ln_b"], np.float32),
        ident=ident.astype(ml_dtypes.bfloat16), identf=ident, tri=tri.astype(ml_dtypes.bfloat16),
    )
    maps = []
    for c in range(NCORES):
        t0 = c * TOK
        xo = np.zeros((NT, D), np.float32)
        if c > 0:
            xo[:HALO] = x[t0 - HALO:t0]
        xo[HALO:] = x[t0:t0 + TOK]
        rc = np.zeros((128, 64), np.float32)
        for gi, w in enumerate((2, 4, 8, 16)):
            tglob = t0 + np.arange(16)
            rc[:, gi * 16:(gi + 1) * 16] = (1.0 / np.minimum(tglob + 1, w)).astype(np.float32)[None, :]
        m = dict(common)
        m.update(
            xo=xo, hflag=np.full((128, 1), 0.0 if c == 0 else 1.0, np.float32), rc=rc,
            pos=np.ascontiguousarray(pos[:, t0:t0 + TOK]),
            wq=np.ascontiguousarray(wqb[:, c * 384:(c + 1) * 384]),
            wkv=np.ascontiguousarray(wkvb[:, c * 512:(c + 1) * 512]),
        )
        maps.append(m)
    return maps


_NC_CACHE = {}


def kernel(**inputs):
    if "nc" not in _NC_CACHE:
        _NC_CACHE["nc"] = build()
    nc = _NC_CACHE["nc"]
    maps = make_in_maps(inputs)
    res = run_bass_kernel_spmd(nc, maps, core_ids=list(range(NCORES)))
    out = np.concatenate([np.asarray(res.results[c]["out"], np.float32) for c in range(NCORES)], axis=0)
    return out.reshape(1, S, D)
```

```python
import math
from contextlib import ExitStack

import ml_dtypes
import numpy as np

import concourse.bass as bass
import concourse.mybir as mybir
from concourse.bass_utils import run_bass_kernel_spmd

F32 = mybir.dt.float32
BF16 = mybir.dt.bfloat16
I32 = mybir.dt.int32
ALU = mybir.AluOpType
AF = mybir.ActivationFunctionType

NCORES = 8
S = 16384
D = 2048
TOK = S // NCORES
HALO = 128
NT = TOK + HALO
INW = 9280
C_QLAT, C_CKV, C_KPE, C_MG, C_PI, C_PG, C_GL = 0, 512, 1024, 1088, 3136, 4160, 5184
LATR = 1088
SCALE = 192.0 ** -0.5
ALPHA = 2.0 ** 0.25
NQT = S // 512
NGRP = 16
GT = S // NGRP
N_ASEM = 28
N_ASEM_SW = 8


class Rec:
    ENGS = ("pe", "act", "dve", "pool", "sp")

    def __init__(self):
        self.ops = []
        self.last_w = {}
        self.readers = {}
        self.last_on = {}
        self.async_since = []
        self.asem_cnt = [0] * N_ASEM
        self.asem_rr = 0
        self.asem_rr_sw = 0
        self.cc_cnt = 0
        self.last_cc = None

    def add(self, eng, fn, r=(), w=(), kind="c"):
        i = len(self.ops)
        deps = set()
        for k in r:
            if k in self.last_w:
                deps.add(self.last_w[k])
        for k in w:
            if k in self.last_w:
                deps.add(self.last_w[k])
            deps.update(self.readers.get(k, ()))
        for k in r:
            self.readers.setdefault(k, []).append(i)
        for k in w:
            self.last_w[k] = i
            self.readers[k] = []
        op = dict(eng=eng, fn=fn, deps=deps, kind=kind, signal=False, sigval=0)
        if kind == "d":
            if eng == "pool":
                s = self.asem_rr_sw
                self.asem_rr_sw = (self.asem_rr_sw + 1) % N_ASEM_SW
            else:
                s = N_ASEM_SW + self.asem_rr
                self.asem_rr = (self.asem_rr + 1) % (N_ASEM - N_ASEM_SW)
            op["aprev"] = 16 * self.asem_cnt[s]
            self.asem_cnt[s] += 1
            op["asem"] = s
            op["aval"] = 16 * self.asem_cnt[s]
            self.async_since.append(i)
        elif kind == "cc":
            if self.last_cc is not None:
                op["deps"].add(self.last_cc)
            self.last_cc = i
            self.cc_cnt += 1
            op["aval"] = self.cc_cnt
            self.async_since.append(i)
        else:
            self.last_on[eng] = i
        self.ops.append(op)
        return i

    def barrier(self):
        lasts = dict(self.last_on)
        asyncs = list(self.async_since)
        for e in self.ENGS:
            deps = set(asyncs)
            for e2, i in lasts.items():
                if e2 != e:
                    deps.add(i)
            self.ops.append(dict(eng=e, fn=None, deps=deps, kind="b", signal=False, sigval=0))
        self.last_w = {}
        self.readers = {}
        self.async_since = []

    def finalize(self):
        ops = self.ops
        for op in ops:
            for d in op["deps"]:
                dop = ops[d]
                if dop["kind"] != "c":
                    continue
                if dop["eng"] == op["eng"] and op["eng"] == "pe" and op["kind"] in ("c", "b"):
                    continue
                dop["signal"] = True
        cnt = {e: 0 for e in self.ENGS}
        for op in ops:
            if op["kind"] == "c" and op["signal"]:
                cnt[op["eng"]] += 1
                op["sigval"] = cnt[op["eng"]]
        self.cnt = cnt

    def emit(self, eng, e, csem, asems, ccsem):
        ops = self.ops
        waited = {}

        def wait(key, sem, val):
            if val <= 0 or waited.get(key, 0) >= val:
                return
            e.wait_ge(sem, val)
            waited[key] = val

        for op in ops:
            if op["eng"] != eng:
                continue
            for d in sorted(op["deps"]):
                dop = ops[d]
                if dop["kind"] == "d":
                    wait(("a", dop["asem"]), asems[dop["asem"]], dop["aval"])
                elif dop["kind"] == "cc":
                    wait("cc", ccsem, dop["aval"])
                elif dop["kind"] == "c":
                    if not dop["signal"]:
                        continue
                    if dop["eng"] == eng and eng == "pe" and op["kind"] in ("c", "b"):
                        continue
                    wait(("c", dop["eng"]), csem[dop["eng"]], dop["sigval"])
            if op["fn"] is None:
                continue
            if op["kind"] == "d":
                wait(("a", op["asem"]), asems[op["asem"]], op["aprev"])
                op["fn"](e).then_inc(asems[op["asem"]], 16)
            elif op["kind"] == "cc":
                op["fn"](e).then_inc(ccsem)
            else:
                ins = op["fn"](e)
                if op["signal"]:
                    ins.then_inc(csem[eng], 1)


class Arena:
    def __init__(self, ap, nbytes):
        self.ap = ap
        self.n = nbytes
        self.top = 0

    def alloc(self, nbytes, dt=F32):
        off = self.top
        self.top += (nbytes + 63) // 64 * 64
        assert self.top <= self.n, f"arena overflow {self.top} > {self.n}"
        a = self.ap[:, off // 4:(off + nbytes) // 4]
        return a if dt == F32 else a.bitcast(dt)

    def mark(self):
        return self.top

    def release(self, m):
        self.top = m


def build(debug=False):
    nc = bass.Bass("TRN2", target_bir_lowering=False)

    def din(name, shape, dt=F32):
        return nc.dram_tensor(name, list(shape), dt, kind="ExternalInput")

    xo = din("xo", [NT, D])
    hflag_d = din("hflag", [128, 1])
    rc_d = din("rc", [128, 64])
    c_d = din("c", [1, D])
    pos_d = din("pos", [1, TOK], I32)
    invf_d = din("invf", [64, 1])
    w_ada = din("w_ada", [D, 3 * D])
    b_ada = din("b_ada", [1, 3 * D])
    w_in = din("w_in", [D, INW])
    b_gates = din("b_gates", [1, 2 * D])
    qng = din("q_norm_g", [1, 512])
    kvng = din("kv_norm_g", [1, 512])
    wq_d = din("wq", [512, 384])
    wkv_d = din("wkv", [512, 512])
    w_mla_o = din("w_mla_o", [D, D])
    w_pool_g = din("w_pool_g", [4, 256, 256])
    pool_scale = din("pool_scale", [1, 1024])
    w_pool_o = din("w_pool_o", [1024, D])
    w_out = din("w_out", [D, D])
    ln_g = din("ln_g", [1, D])
    ln_b = din("ln_b", [1, D])
    ident_d = din("ident", [128, 128], BF16)
    identf_d = din("identf", [128, 128])
    tri_d = din("tri", [128, 128], BF16)
    out_d = nc.dram_tensor("out", [TOK, D], F32, kind="ExternalOutput")

    lat_in = nc.dram_tensor("lat_in", [LATR, TOK], BF16)
    lat_out = nc.dram_tensor("lat_out", [NCORES * LATR, TOK], BF16)
    cs_in = nc.dram_tensor("cs_in", [128, TOK], F32)
    cs_out = nc.dram_tensor("cs_out", [NCORES * 128, TOK], F32)
    sg_d = nc.dram_tensor("sg_d", [128, 16 * TOK], BF16)
    gm_d = nc.dram_tensor("gm_d", [128, 16 * TOK], BF16)
    zp_d = nc.dram_tensor("zp_d", [128, 16 * TOK], BF16)
    g1_d = nc.dram_tensor("g1_d", [1, D], F32)
    yb_in = [nc.dram_tensor(f"yb_in{g}", [256, GT], F32) for g in range(NGRP)]
    yb_out = nc.dram_tensor("yb_out", [NGRP * D, GT], F32)
    ymine = nc.dram_tensor("ymine", [2 * D, GT], F32)

    dbg = {}
    if debug:
        dbg["lat"] = nc.dram_tensor("dbg_lat", [LATR, TOK], BF16, kind="ExternalOutput")
        dbg["cs"] = nc.dram_tensor("dbg_cs", [128, TOK], F32, kind="ExternalOutput")
        dbg["sg"] = nc.dram_tensor("dbg_sg", [128, 16 * TOK], BF16, kind="ExternalOutput")
        dbg["gm"] = nc.dram_tensor("dbg_gm", [128, 16 * TOK], BF16, kind="ExternalOutput")
        dbg["zp"] = nc.dram_tensor("dbg_zp", [128, 16 * TOK], BF16, kind="ExternalOutput")
        dbg["y"] = nc.dram_tensor("dbg_y", [NGRP * 256, GT], F32, kind="ExternalOutput")

    es = ExitStack()
    with es:
        ARENA_BYTES = 212480
        arena_t = es.enter_context(nc.sbuf_tensor("arena", [128, ARENA_BYTES // 4], F32))
        psum_t = es.enter_context(nc.psum_tensor("psum", [128, 8 * 512], F32))
        csem = {e: es.enter_context(nc.semaphore("cs_" + e)) for e in ("pe", "act", "dve", "pool")}
        asems = [es.enter_context(nc.semaphore(f"as{i}")) for i in range(N_ASEM)]
        ccsem = es.enter_context(nc.semaphore("ccs"))
        A = Arena(arena_t[:, :], ARENA_BYTES)
        ps = [psum_t[:, b * 512:(b + 1) * 512] for b in range(8)]
        R = Rec()
        PID = [None]

        def PS(b):
            return ("ps", b)

        def dma(eng, out, in_, r=(), w=()):
            R.add(eng, lambda e: e.dma_start(out=out, in_=in_), r, w, kind="d")

        def mm(out, lhsT, rhs, start, stop, r, w):
            R.add("pe", lambda e: e.matmul(out, lhsT, rhs, start=start, stop=stop), r, w)

        def act(out, in_, func, r, w, bias=None, scale=None, accum=None):
            kw = {}
            if bias is not None:
                kw["bias"] = bias
            if scale is not None:
                kw["scale"] = scale
            if accum is not None:
                kw["accum_out"] = accum
            R.add("act", lambda e: e.activation(out, in_, func, **kw), r, w)

        def ts2(eng, out, in0, s1, s2, op0, op1, r, w):
            if op1 is None:
                R.add(eng, lambda e: e.tensor_scalar(out, in0, s1, None, op0), r, w)
            else:
                R.add(eng, lambda e: e.tensor_scalar(out, in0, s1, s2, op0, op1), r, w)

        def tt(eng, out, in0, in1, op, r, w):
            R.add(eng, lambda e: e.tensor_tensor(out, in0, in1, op), r, w)

        def stt(eng, out, in0, scalar, in1, op0, op1, r, w):
            R.add(eng, lambda e: e.scalar_tensor_tensor(out, in0, scalar, in1, op0, op1), r, w)

        def cp(eng, out, in_, r, w):
            R.add(eng, lambda e: e.tensor_copy(out, in_), r, w)

        def memset(eng, ap, val, w):
            R.add(eng, lambda e: e.memset(ap, val), (), w)

        ident = A.alloc(256, BF16)
        identf = A.alloc(512, F32)
        tri = A.alloc(256, BF16)
        onesf = A.alloc(512, F32)
        colv = A.alloc(256, F32)
        scb = A.alloc(32, BF16)
        shiftc = A.alloc(64, F32)
        scale1c = A.alloc(64, F32)
        epsc = A.alloc(16, F32)
        hflag = A.alloc(4, F32)
        rc = A.alloc(256, F32)
        invf = A.alloc(4, F32)
        small = A.alloc(64, F32)
        dma("sp", ident, ident_d[:, :], w=["ident"])
        dma("sp", identf, identf_d[:, :], w=["identf"])
        dma("sp", tri, tri_d[:, :], w=["tri"])
        dma("sp", hflag, hflag_d[:, :], w=["hflag"])
        dma("sp", rc, rc_d[:, :], w=["rc"])
        dma("sp", invf[0:64, :], invf_d[:, :], w=["invf"])
        memset("dve", onesf, 1.0, ["onesf"])
        memset("dve", epsc[:, 0:1], 1e-5, ["epsc"])
        memset("dve", epsc[:, 1:2], 1e-6, ["epsc"])
        memset("dve", epsc[:, 2:3], -math.pi, ["epsc"])
        memset("dve", epsc[:, 3:4], 0.0, ["epsc"])

        P_TOP = A.mark()
        hT = A.alloc(16 * NT * 2, BF16).rearrange("p (k t) -> p k t", k=16)
        wbuf = [A.alloc(16 * 512 * 2, BF16).rearrange("p (k n) -> p k n", k=16) for _ in range(2)]
        wb_i = [0]
        M1 = A.mark()

        def load_w(pieces):
            s = wb_i[0] % 2
            wb_i[0] += 1
            for (src, c0, n, dc) in pieces:
                dma("pool", wbuf[s][:, :, dc:dc + n],
                    src[:, c0:c0 + n].rearrange("(k p) n -> p k n", p=128), w=[("wb", s)])
            return s

        sv = A.alloc(512, F32)
        brow = A.alloc(6144 * 4, F32)
        modrow = A.alloc(6144 * 4, F32)
        dma("sp", sv[0:16, :], c_d[:, :].rearrange("o (k p) -> (o k) p", p=128), w=["sv"])
        dma("sp", sv[16:20, :], qng[:, :].rearrange("o (k p) -> (o k) p", p=128), w=["sv"])
        dma("sp", sv[20:24, :], kvng[:, :].rearrange("o (k p) -> (o k) p", p=128), w=["sv"])
        dma("sp", sv[24:32, :], pool_scale[:, :].rearrange("o (k p) -> (o k) p", p=128), w=["sv"])
        dma("sp", sv[32:64, :], b_gates[:, :].rearrange("o (k p) -> (o k) p", p=128), w=["sv"])
        dma("sp", brow[0:1, :], b_ada[:, :], w=["brow"])
        act(sv[0:16, :], sv[0:16, :], AF.Silu, r=["sv"], w=["sv"])
        mm(ps[0][:, 0:64], sv[0:64, :], identf[0:64, 0:64], True, True, r=["sv", "identf"], w=[PS(0)])
        cp("dve", colv, ps[0][:, 0:64], r=[PS(0)], w=["colv"])
        cp("dve", scb, colv[:, 0:16], r=["colv"], w=["scb"])
        qgc, kvgc, pscc, bgc = colv[:, 16:20], colv[:, 20:24], colv[:, 24:32], colv[:, 32:64]
        for n in range(12):
            s = load_w([(w_ada, n * 512, 512, 0)])
            b = 1 + n % 2
            for k in range(16):
                mm(ps[b][0:1, :], scb[:, k:k + 1], wbuf[s][:, k, :], k == 0, k == 15,
                   r=["scb", ("wb", s)], w=[PS(b)])
            tt("dve", modrow[0:1, n * 512:(n + 1) * 512], ps[b][0:1, :], brow[0:1, n * 512:(n + 1) * 512],
               ALU.add, r=[PS(b), "brow"], w=["modrow"])
        for i in range(32):
            mm(ps[3][:, i:i + 1], modrow[0:1, i * 128:(i + 1) * 128], identf[0:1, 0:1], True, True,
               r=["modrow", "identf"], w=[PS(3)])
        cp("dve", shiftc, ps[3][:, 0:16], r=[PS(3)], w=["shiftc"])
        ts2("dve", scale1c, ps[3][:, 16:32], 1.0, None, ALU.add, None, r=[PS(3)], w=["scale1c"])
        ts2("dve", modrow[0:1, 4096:6144], modrow[0:1, 4096:6144], 1.0, None, ALU.add, None,
            r=["modrow"], w=["modrow"])
        dma("sp", g1_d[:, :], modrow[0:1, 4096:6144], r=["modrow"], w=["g1_d"])
        R.barrier()
        A.release(M1)

        xt = [A.alloc(D * 4, F32) for _ in range(2)]
        xn = [A.alloc(D * 2, BF16) for _ in range(2)]
        junk = A.alloc(D * 4, F32)
        for t in range(NT // 128):
            s = t % 2
            sm = small[:, s * 8:(s + 1) * 8]
            SM = ("sm", s)
            dma("sp", xt[s], xo[t * 128:(t + 1) * 128, :], w=[("xt", s)])
            memset("dve", sm[:, 0:3], 0.0, [SM])
            act(junk, xt[s], AF.Identity, r=[("xt", s), SM], w=["junk", SM], accum=sm[:, 0:1])
            ts2("dve", sm[:, 1:2], sm[:, 0:1], -1.0 / D, None, ALU.mult, None, r=[SM], w=[SM])
            act(junk, xt[s], AF.Square, r=[("xt", s), SM], w=["junk", SM], bias=sm[:, 1:2], accum=sm[:, 2:3])
            act(sm[:, 3:4], sm[:, 2:3], AF.Sqrt, r=[SM, "epsc"], w=[SM], bias=epsc[:, 0:1], scale=1.0 / D)
            R.add("dve", lambda e, o=sm[:, 4:5], i=sm[:, 3:4]: e.reciprocal(o, i), [SM], [SM])
            tt("dve", sm[:, 5:6], sm[:, 1:2], sm[:, 4:5], ALU.mult, r=[SM], w=[SM])
            act(xn[s], xt[s], AF.Identity, r=[("xt", s), SM], w=[("xn", s)], bias=sm[:, 5:6], scale=sm[:, 4:5])
            b0 = 2 * (t % 2)
            for k in range(16):
                pb = ps[b0 + k // 8].bitcast(BF16)
                R.add("pe", lambda e, o=pb[:, (k % 8) * 128:(k % 8 + 1) * 128], i=xn[s][:, k * 128:(k + 1) * 128]:
                      e.transpose(o, i, ident), [("xn", s), "ident"], [PS(b0 + k // 8)])
            for k in range(16):
                pb = ps[b0 + k // 8].bitcast(BF16)
                ts2("dve", hT[:, k, t * 128:(t + 1) * 128], pb[:, (k % 8) * 128:(k % 8 + 1) * 128],
                    scale1c[:, k:k + 1], shiftc[:, k:k + 1], ALU.mult, ALU.add,
                    r=[PS(b0 + k // 8), "scale1c", "shiftc"], w=[("hT", t)])
            if t == 0:
                ts2("dve", hT[:, :, 0:128], hT[:, :, 0:128], hflag[:, 0:1], None, ALU.mult, None,
                    r=[("hT", 0), "hflag"], w=[("hT", 0)])
        R.barrier()
        A.release(M1)
        HT_ALL = [("hT", t) for t in range(NT // 128)]

        bank_rr = [0]

        def nbank():
            b = bank_rr[0] % 8
            bank_rr[0] += 1
            return b

        def proj(lhs_fn, M, c0, n, consumer, nk=16, rhs_fn=None, rkeys=None):
            b = nbank()
            for k in range(nk):
                rhs = hT[:, k, c0:c0 + n] if rhs_fn is None else rhs_fn(k)
                mm(ps[b][0:M, 0:n], lhs_fn(k), rhs, k == 0, k == nk - 1,
                   r=(rkeys if rkeys is not None else HT_ALL + [("wb", 0), ("wb", 1)]), w=[PS(b)])
            consumer(b)

        CS = A.alloc(2 * TOK * 4, F32).rearrange("p (c t) -> p c t", c=2)
        posi = A.alloc(TOK * 4, I32)
        ang = A.alloc(TOK * 4, F32)
        rr = A.alloc(TOK * 4, F32)
        dma("sp", posi[0:64, :], pos_d[:, :].partition_broadcast(64), w=["posi"])
        cp("dve", ang[0:64, :], posi[0:64, :], r=["posi"], w=["ang"])
        ts2("dve", ang[0:64, :], ang[0:64, :], invf[0:64, 0:1], None, ALU.mult, None, r=["ang", "invf"], w=["ang"])
        C1 = 6.28125
        C2 = 2 * math.pi - C1
        qi = posi

        def reduce_angle(src_add):
            ts2("dve", rr[0:64, :], ang[0:64, :], src_add, 1.0 / (2 * math.pi), ALU.add, ALU.mult, r=["ang", "CS"], w=["rr"])
            cp("dve", qi[0:64, :], rr[0:64, :], r=["rr"], w=["posi"])
            cp("dve", rr[0:64, :], qi[0:64, :], r=["posi"], w=["rr"])
            stt("dve", t2r[0:64, :], rr[0:64, :], -C1, ang[0:64, :], ALU.mult, ALU.add, r=["rr", "ang"], w=["t2r"])
            stt("dve", t2r[0:64, :], rr[0:64, :], -C2, t2r[0:64, :], ALU.mult, ALU.add, r=["rr", "t2r"], w=["t2r"])
            if src_add != 0.0:
                ts2("dve", t2r[0:64, :], t2r[0:64, :], src_add, None, ALU.add, None, r=["t2r"], w=["t2r"])
            ts2("dve", rr[0:64, :], t2r[0:64, :], math.pi, 2 * math.pi, ALU.is_gt, ALU.mult, r=["t2r"], w=["rr"])
            tt("dve", t2r[0:64, :], t2r[0:64, :], rr[0:64, :], ALU.subtract, r=["t2r", "rr"], w=["t2r"])
            ts2("dve", rr[0:64, :], t2r[0:64, :], -math.pi, 2 * math.pi, ALU.is_lt, ALU.mult, r=["t2r"], w=["rr"])
            tt("dve", rr[0:64, :], t2r[0:64, :], rr[0:64, :], ALU.add, r=["t2r", "rr"], w=["rr"])

        t2r = A.alloc(TOK * 4, F32)
        reduce_angle(0.0)
        act(CS[0:64, 1, :], rr[0:64, :], AF.Sin, r=["rr"], w=["CS"])
        ts2("dve", CS[0:32, 1, :], CS[0:32, 1, :], -1.0, None, ALU.mult, None, r=["CS"], w=["CS"])
        reduce_angle(math.pi / 2)
        act(CS[0:64, 0, :], rr[0:64, :], AF.Sin, r=["rr"], w=["CS"])
        dma("sp", cs_in[0:64, :], CS[0:64, 0, :], r=["CS"], w=["cs_in"])
        dma("sp", cs_in[64:128, :], CS[0:64, 1, :], r=["CS"], w=["cs_in"])

        ql = A.alloc(4 * TOK * 4, F32).rearrange("p (k t) -> p k t", k=4)
        sq = [A.alloc(512 * 4, F32) for _ in range(2)]
        rstd = A.alloc(512 * 4, F32)
        stg = [A.alloc(TOK * 2, BF16) for _ in range(2)]
        stg_i = [0]
        t1 = A.alloc(512 * 4, F32)
        t2 = A.alloc(512 * 4, F32)

        for li, (c0w, gcol, row0) in enumerate(((C_QLAT, qgc, 0), (C_CKV, kvgc, 512))):
            s = load_w([(w_in, c0w, 512, 0)])
            SSQ = 6 + li
            for tt_ in range(4):
                c0 = HALO + tt_ * 512
                for m in range(4):
                    def cons(b, m=m, tt_=tt_):
                        act(ql[:, m, tt_ * 512:(tt_ + 1) * 512], ps[b], AF.Copy, r=[PS(b)], w=[("ql", tt_)])
                        act(sq[m % 2], ps[b], AF.Square, r=[PS(b)], w=[("sq", m % 2)])
                        mm(ps[SSQ], onesf, sq[m % 2], m == 0, m == 3, r=[("sq", m % 2), "onesf"], w=[PS(SSQ)])
                    b = bank_rr[0] % 6
                    bank_rr[0] += 1
                    for k in range(16):
                        mm(ps[b][:, :], wbuf[s][:, k, m * 128:(m + 1) * 128], hT[:, k, c0:c0 + 512],
                           k == 0, k == 15, r=HT_ALL + [("wb", s)], w=[PS(b)])
                    cons(b)
                act(rstd, ps[SSQ], AF.Sqrt, r=[PS(SSQ), "epsc"], w=["rstd"], bias=epsc[:, 1:2], scale=1.0 / 512)
                R.add("dve", lambda e: e.reciprocal(rstd, rstd), ["rstd"], ["rstd"])
                for m in range(4):
                    si = stg_i[0] % 2
                    stg_i[0] += 1
                    stt("dve", stg[si][:, 0:512], ql[:, m, tt_ * 512:(tt_ + 1) * 512], gcol[:, m:m + 1], rstd,
                        ALU.mult, ALU.mult, r=[("ql", tt_), "rstd", "colv"], w=[("stg", si)])
                    dma("sp", lat_in[row0 + m * 128:row0 + (m + 1) * 128, tt_ * 512:(tt_ + 1) * 512],
                        stg[si][:, 0:512], r=[("stg", si)], w=[("lat_in", row0 + m * 128, tt_)])
        s = load_w([(w_in, C_KPE, 64, 0), (w_in, C_KPE + 32, 32, 64), (w_in, C_KPE, 32, 96)])
        for tt_ in range(4):
            c0 = HALO + tt_ * 512
            bA = bank_rr[0] % 6
            bB = (bank_rr[0] + 1) % 6
            bank_rr[0] += 2
            for k in range(16):
                mm(ps[bA][0:64, :], wbuf[s][:, k, 0:64], hT[:, k, c0:c0 + 512], k == 0, k == 15,
                   r=HT_ALL + [("wb", s)], w=[PS(bA)])
            for k in range(16):
                mm(ps[bB][0:64, :], wbuf[s][:, k, 64:128], hT[:, k, c0:c0 + 512], k == 0, k == 15,
                   r=HT_ALL + [("wb", s)], w=[PS(bB)])
            tt("dve", t1[0:64, :], ps[bA][0:64, :], CS[0:64, 0, tt_ * 512:(tt_ + 1) * 512], ALU.mult,
               r=[PS(bA), "CS"], w=["t1"])
            tt("dve", t2[0:64, :], ps[bB][0:64, :], CS[0:64, 1, tt_ * 512:(tt_ + 1) * 512], ALU.mult,
               r=[PS(bB), "CS"], w=["t2"])
            si = stg_i[0] % 2
            stg_i[0] += 1
            tt("dve", stg[si][0:64, 0:512], t1[0:64, :], t2[0:64, :], ALU.add, r=["t1", "t2"], w=[("stg", si)])
            dma("sp", lat_in[1024:1088, tt_ * 512:(tt_ + 1) * 512], stg[si][0:64, 0:512],
                r=[("stg", si)], w=[("lat_in", 1024, tt_)])
        R.add("pool", lambda e: e.collective_compute("AllGather", ALU.bypass, replica_groups=[list(range(NCORES))],
                                                     ins=[lat_in.ap().opt()], outs=[lat_out.ap().opt()]),
              [("lat_in", r0, t_) for r0 in range(0, 1088, 128) for t_ in range(4)], ["lat_out"], kind="cc")
        R.add("pool", lambda e: e.collective_compute("AllGather", ALU.bypass, replica_groups=[list(range(NCORES))],
                                                     ins=[cs_in.ap().opt()], outs=[cs_out.ap().opt()]),
              ["cs_in"], ["cs_out"], kind="cc")
        if debug:
            dma("sp", dbg["lat"][:, :], lat_in[:, :],
                r=[("lat_in", r0, t_) for r0 in range(0, 1088, 128) for t_ in range(4)], w=["dbg_lat"])
            dma("sp", dbg["cs"][:, :], cs_in[:, :], r=["cs_in"], w=["dbg_cs"])
        R.barrier()
        A.release(M1)

        stg = [A.alloc(TOK * 2, BF16) for _ in range(2)]
        for (cbase, func, dst, dname, bias_c0) in ((C_MG, AF.Silu, sg_d, "sg_d", None),
                                                   (C_GL, AF.Sigmoid, gm_d, "gm_d", 0)):
            for grp in range(4):
                s = load_w([(w_in, cbase + grp * 512, 512, 0)])
                for m4 in range(4):
                    m = grp * 4 + m4
                    si = stg_i[0] % 2
                    stg_i[0] += 1
                    for tt_ in range(4):
                        def cons(b, tt_=tt_, si=si, m=m):
                            if bias_c0 is None:
                                act(stg[si][:, tt_ * 512:(tt_ + 1) * 512], ps[b], func, r=[PS(b)], w=[("stg", si)])
                            else:
                                act(stg[si][:, tt_ * 512:(tt_ + 1) * 512], ps[b], func, r=[PS(b), "colv"],
                                    w=[("stg", si)], bias=bgc[:, bias_c0 + m:bias_c0 + m + 1])
                        proj(lambda k, s=s, m4=m4: wbuf[s][:, k, m4 * 128:(m4 + 1) * 128], 128,
                             HALO + tt_ * 512, 512, cons, rkeys=HT_ALL + [("wb", s)])
                    dma("sp", dst[:, m * TOK:(m + 1) * TOK], stg[si], r=[("stg", si)], w=[(dname, m)])

        wpg = A.alloc(8 * 256 * 2, BF16).rearrange("p (j n) -> p j n", j=8)
        for g in range(4):
            for ic in range(2):
                dma("pool", wpg[:, g * 2 + ic, :], w_pool_g[g, ic * 128:(ic + 1) * 128, :], w=["wpg"])
        u = A.alloc(NT * 4, F32)
        sA = A.alloc(NT * 4, F32)
        sB = A.alloc(NT * 4, F32)
        pl = A.alloc(2 * TOK * 2, BF16).rearrange("p (c t) -> p c t", c=2)
        sgp = A.alloc(2 * TOK * 2, BF16).rearrange("p (c t) -> p c t", c=2)
        pm = A.alloc(8 * TOK * 2, BF16).rearrange("p (k t) -> p k t", k=8)
        tmp16 = A.alloc(64, F32)
        for g in range(4):
            wdw = 2 ** (g + 1)
            s = load_w([(w_in, C_PI + g * 256, 256, 0), (w_in, C_PG + g * 256, 256, 256)])
            for c in range(2):
                def cons_h(b):
                    act(u[:, 0:128], ps[b][:, 0:128], AF.Copy, r=[PS(b)], w=["u"])
                proj(lambda k, s=s, c=c: wbuf[s][:, k, c * 128:(c + 1) * 128], 128, 0, 128, cons_h,
                     rkeys=HT_ALL + [("wb", s)])
                for tt_ in range(4):
                    def cons(b, tt_=tt_):
                        act(u[:, HALO + tt_ * 512:HALO + (tt_ + 1) * 512], ps[b], AF.Copy, r=[PS(b)], w=["u"])
                    proj(lambda k, s=s, c=c: wbuf[s][:, k, c * 128:(c + 1) * 128], 128, HALO + tt_ * 512, 512,
                         cons, rkeys=HT_ALL + [("wb", s)])
                tt("dve", sA[:, 1:NT], u[:, 1:NT], u[:, 0:NT - 1], ALU.add, r=["u"], w=["sA"])
                cur, oth, ck, ok_ = sA, sB, "sA", "sB"
                sh = 2
                while sh < wdw:
                    lo = 2 * sh - 1
                    tt("dve", oth[:, lo:NT], cur[:, lo:NT], cur[:, lo - sh:NT - sh], ALU.add, r=[ck], w=[ok_])
                    cur, oth, ck, ok_ = oth, cur, ok_, ck
                    sh *= 2
                stt("dve", pl[:, c, :], cur[:, HALO:NT], 1.0 / wdw, u[:, HALO:NT], ALU.mult, ALU.subtract,
                    r=[ck, "u"], w=["pl"])
                tt("dve", tmp16[:, 0:16], cur[:, HALO:HALO + 16], rc[:, g * 16:(g + 1) * 16], ALU.mult,
                   r=[ck, "rc"], w=["tmp16"])
                tt("dve", pl[:, c, 0:16], tmp16[:, 0:16], u[:, HALO:HALO + 16], ALU.subtract,
                   r=["tmp16", "u", "pl"], w=["pl"])
                for tt_ in range(4):
                    def cons(b, tt_=tt_, c=c):
                        act(sgp[:, c, tt_ * 512:(tt_ + 1) * 512], ps[b], AF.Silu, r=[PS(b)], w=["sgp"])
                    proj(lambda k, s=s, c=c: wbuf[s][:, k, 256 + c * 128:256 + (c + 1) * 128], 128,
                         HALO + tt_ * 512, 512, cons, rkeys=HT_ALL + [("wb", s)])
            for oc in range(2):
                for tt_ in range(4):
                    def cons(b, tt_=tt_, oc=oc, g=g):
                        stt("dve", pm[:, g * 2 + oc, tt_ * 512:(tt_ + 1) * 512], ps[b],
                            pscc[:, g * 2 + oc:g * 2 + oc + 1], sgp[:, oc, tt_ * 512:(tt_ + 1) * 512],
                            ALU.mult, ALU.mult, r=[PS(b), "sgp", "colv"], w=["pm"])
                    proj(lambda ic, g=g, oc=oc: wpg[:, g * 2 + ic, oc * 128:(oc + 1) * 128], 128, 0, 512, cons,
                         nk=2, rhs_fn=lambda ic, tt_=tt_: pl[:, ic, tt_ * 512:(tt_ + 1) * 512],
                         rkeys=["wpg", "pl"])
        wpo = [A.alloc(8 * 512 * 2, BF16).rearrange("p (k n) -> p k n", k=8) for _ in range(1)]
        gp = [A.alloc(512 * 4, F32) for _ in range(2)]
        gp_i = 0
        for grp in range(4):
            s = load_w([(w_in, C_GL + D + grp * 512, 512, 0)])
            ws = 0
            dma("pool", wpo[ws], w_pool_o[:, grp * 512:(grp + 1) * 512].rearrange("(k p) n -> p k n", p=128),
                w=[("wpo", ws)])
            for m4 in range(4):
                m = grp * 4 + m4
                si = stg_i[0] % 2
                stg_i[0] += 1
                for tt_ in range(4):
                    gi = gp_i % 2
                    gp_i += 1

                    def cons_g(b, gi=gi, m=m):
                        act(gp[gi], ps[b], AF.Sigmoid, r=[PS(b), "colv"], w=[("gp", gi)],
                            bias=bgc[:, 16 + m:16 + m + 1])
                    proj(lambda k, s=s, m4=m4: wbuf[s][:, k, m4 * 128:(m4 + 1) * 128], 128,
                         HALO + tt_ * 512, 512, cons_g, rkeys=HT_ALL + [("wb", s)])

                    def cons_z(b, gi=gi, si=si, tt_=tt_):
                        tt("dve", stg[si][:, tt_ * 512:(tt_ + 1) * 512], ps[b], gp[gi], ALU.mult,
                           r=[PS(b), ("gp", gi)], w=[("stg", si)])
                    proj(lambda k8, ws=ws, m4=m4: wpo[ws][:, k8, m4 * 128:(m4 + 1) * 128], 128, 0, 512, cons_z,
                         nk=8, rhs_fn=lambda k8, tt_=tt_: pm[:, k8, tt_ * 512:(tt_ + 1) * 512],
                         rkeys=["pm", ("wpo", ws)])
                dma("sp", zp_d[:, m * TOK:(m + 1) * TOK], stg[si], r=[("stg", si)], w=[("zp_d", m)])
        R.barrier()
        A.release(P_TOP)
        if debug:
            dma("sp", dbg["sg"][:, :], sg_d[:, :], w=["dbg_sg"])
            dma("sp", dbg["gm"][:, :], gm_d[:, :], w=["dbg_gm"])
            dma("sp", dbg["zp"][:, :], zp_d[:, :], w=["dbg_zp"])

        KT = A.alloc(2 * S * 2, BF16).rearrange("p (a t) -> p a t", a=2)
        V = A.alloc(128 * 256 * 2, BF16).rearrange("p (b n) -> p b n", b=128)
        KPE = A.alloc(S * 2, BF16)
        wq = A.alloc(4 * 512 * 2, BF16).rearrange("p (k n) -> p k n", k=4)
        wkv = A.alloc(4 * 512 * 2, BF16).rearrange("p (k n) -> p k n", k=4)
        qlat = [A.alloc(4 * 512 * 2, BF16).rearrange("p (k n) -> p k n", k=4) for _ in range(1)]
        ckv = [A.alloc(4 * 512 * 2, BF16).rearrange("p (k n) -> p k n", k=4) for _ in range(1)]
        cst = [A.alloc(2 * 512 * 4, F32).rearrange("p (c n) -> p c n", c=2) for _ in range(1)]
        QN = A.alloc(2 * 512 * 2, BF16).rearrange("p (a n) -> p a n", a=2)
        QR = A.alloc(512 * 2, BF16)
        NPT = 3
        pT2f = [A.alloc(2 * 512 * 2, BF16) for _ in range(NPT)]
        pT2 = [t_.rearrange("p (a n) -> p a n", a=2) for t_ in pT2f]
        accf = A.alloc(2 * 512 * 4, F32)
        acc = accf.rearrange("p (a n) -> p a n", a=2)
        tmpb = A.alloc(2 * 512 * 2, BF16)
        rcp = A.alloc(512 * 4, F32)
        yT = [A.alloc(512 * 4, F32) for _ in range(2)]

        def kmaj(src_rows):
            return src_rows.rearrange("(k p) n -> p k n", p=128)

        for (c0, n, dc) in ((0, 128, 0), (192, 128, 128), (128, 64, 256), (320, 64, 320),
                            (160, 32, 384), (128, 32, 416), (352, 32, 448), (320, 32, 480)):
            dma("pool", wq[:, :, dc:dc + n], kmaj(wq_d[:, c0:c0 + n]), w=["wq"])
        for (c0, n, dc) in ((0, 128, 0), (256, 128, 128), (128, 128, 256), (384, 128, 384)):
            dma("pool", wkv[:, :, dc:dc + n], kmaj(wkv_d[:, c0:c0 + n]), w=["wkv"])

        step_i = [0]
        yt_i = [0]

        def load_tile(j):
            s = 0
            rk, t0 = j // 4, (j % 4) * 512
            base = rk * LATR
            dma("sp", qlat[s], kmaj(lat_out[base:base + 512, t0:t0 + 512]), r=["lat_out"], w=[("qlat", s)])
            dma("sp", ckv[s], kmaj(lat_out[base + 512:base + 1024, t0:t0 + 512]), r=["lat_out"], w=[("ckv", s)])
            for h in range(2):
                dma("sp", KPE[h * 64:(h + 1) * 64, j * 512:(j + 1) * 512],
                    lat_out[base + 1024:base + 1088, t0:t0 + 512], r=["lat_out"], w=[("KPE", j)])
                for c in range(2):
                    dma("sp", cst[s][h * 64:(h + 1) * 64, c, :],
                        cs_out[rk * 128 + c * 64:rk * 128 + (c + 1) * 64, t0:t0 + 512],
                        r=["cs_out"], w=[("cst", s)])

        def kv_proj(j, part):
            s = 0
            if part < 2:
                a = part
                for k in range(4):
                    mm(ps[6 + a], wkv[:, k, a * 128:(a + 1) * 128], ckv[s][:, k, :], k == 0, k == 3,
                       r=["wkv", ("ckv", s)], w=[PS(6 + a)])
                act(KT[:, a, j * 512:(j + 1) * 512], ps[6 + a], AF.Copy, r=[PS(6 + a)], w=[("KT", j)])
            else:
                half = part - 2
                bank = 6 + half
                for tb2 in range(2):
                    tb = half * 2 + tb2
                    for k in range(4):
                        mm(ps[bank][:, tb2 * 256:(tb2 + 1) * 256], ckv[s][:, k, tb * 128:(tb + 1) * 128],
                           wkv[:, k, 256:512], k == 0, k == 3, r=["wkv", ("ckv", s)], w=[PS(bank)])
                cp("dve", V[:, 4 * j + half * 2:4 * j + half * 2 + 2, :],
                   ps[bank].rearrange("p (b n) -> p b n", b=2), r=[PS(bank)], w=[("V", j)])

        def q_proj(j):
            s = 0
            for v_ in range(2):
                for k in range(4):
                    mm(ps[6 + v_], wq[:, k, 256 + v_ * 128:256 + (v_ + 1) * 128], qlat[s][:, k, :], k == 0, k == 3,
                       r=["wq", ("qlat", s)], w=[PS(6 + v_)])
            tt("dve", yT[0], ps[6], cst[s][:, 0, :], ALU.mult, r=[PS(6), ("cst", s)], w=[("yT", 0)])
            tt("dve", rcp, ps[7], cst[s][:, 1, :], ALU.mult, r=[PS(7), ("cst", s)], w=["rcp"])
            tt("dve", QR, yT[0], rcp, ALU.add, r=[("yT", 0), "rcp"], w=["QR"])
            for a in range(2):
                for k in range(4):
                    mm(ps[6 + a], wq[:, k, a * 128:(a + 1) * 128], qlat[s][:, k, :], k == 0, k == 3,
                       r=["wq", ("qlat", s)], w=[PS(6 + a)])
                act(QN[:, a, :], ps[6 + a], AF.Copy, r=[PS(6 + a)], w=["QN"])

        load_tile(0)
        for part in range(4):
            kv_proj(0, part)
        for j in range(NQT):
            q_proj(j)
            if j + 1 < NQT:
                load_tile(j + 1)

            nkb = 4 * j + 4
            info = {}

            def issue_S(kb):
                q0 = max(0, kb - 4 * j) * 128
                g_i = step_i[0]
                step_i[0] += 1
                sbp, slot = (g_i % 2) * 2, g_i % NPT
                info[kb] = (q0, slot)
                for a in range(2):
                    mm(ps[sbp + a][:, q0:512], KT[:, a, kb * 128:(kb + 1) * 128], QN[:, a, q0:512], True, False,
                       r=[("KT", kb // 4), "QN"], w=[PS(sbp + a)])
                for a in range(2):
                    mm(ps[sbp + a][:, q0:512], KPE[a * 64:(a + 1) * 64, kb * 128:(kb + 1) * 128],
                       QR[a * 64:(a + 1) * 64, q0:512], False, True, r=[("KPE", kb // 4), "QR"],
                       w=[PS(sbp + a)] if a == 0 else [PS(sbp), PS(sbp + 1)])
                s_in = psum_t[:, sbp * 512:(sbp + 2) * 512].rearrange("p (a n) -> p a n", a=2)
                act(pT2[slot][:, :, q0:512], s_in[:, :, q0:512], AF.Exp, r=[PS(sbp), PS(sbp + 1)],
                    w=[("pT", slot)], scale=SCALE)
                if kb >= 4 * j:
                    for a in range(2):
                        tt("dve", pT2[slot][:, a, q0:q0 + 128], pT2[slot][:, a, q0:q0 + 128], tri, ALU.mult,
                           r=[("pT", slot), "tri"], w=[("pT", slot)])
                    if kb == 0:
                        cp("dve", accf, pT2f[slot], r=[("pT", slot)], w=["acc"])
                    else:
                        tt("dve", acc[:, :, q0:512], acc[:, :, q0:512], pT2[slot][:, :, q0:512], ALU.add,
                           r=[("pT", slot), "acc"], w=["acc"])
                elif kb % 2 == 1:
                    s0 = info[kb - 1][1]
                    if kb == 1:
                        tt("dve", accf, pT2f[s0], pT2f[slot], ALU.add, r=[("pT", s0), ("pT", slot)], w=["acc"])
                    else:
                        tt("dve", tmpb, pT2f[s0], pT2f[slot], ALU.add, r=[("pT", s0), ("pT", slot)], w=["tmpb"])
                        tt("dve", accf, accf, tmpb, ALU.add, r=["tmpb", "acc"], w=["acc"])

            def issue_PV(kb):
                q0, slot = info[kb]
                for a in range(2):
                    mm(ps[4 + a][:, q0:512], V[:, kb, a * 128:(a + 1) * 128], pT2[slot][:, a, q0:512],
                       kb == 0, kb == nkb - 1, r=[("V", kb // 4), ("pT", slot)], w=[PS(4 + a)])

            LOOK = 1
            kv_done = 0
            for idx in range(nkb + LOOK):
                if idx < nkb:
                    issue_S(idx)
                if idx >= LOOK:
                    issue_PV(idx - LOOK)
                if j + 1 < NQT and 0 <= idx - nkb // 2 < 4:
                    kv_proj(j + 1, idx - nkb // 2)
                    kv_done = idx - nkb // 2 + 1
            if j + 1 < NQT:
                for part in range(kv_done, 4):
                    kv_proj(j + 1, part)
            g = j // 2
            for a in range(2):
                mm(ps[6 + a], onesf, acc[:, a, :], True, True, r=["acc", "onesf"], w=[PS(6 + a)])
                R.add("dve", lambda e, i=ps[6 + a]: e.reciprocal(rcp, i), [PS(6 + a)], ["rcp"])
                yi = yt_i[0] % 2
                yt_i[0] += 1
                tt("dve", yT[yi], ps[4 + a], rcp, ALU.mult, r=[PS(4 + a), "rcp"], w=[("yT", yi)])
                dma("sp", yb_in[g][a * 128:(a + 1) * 128, (j % 2) * 512:(j % 2 + 1) * 512], yT[yi],
                    r=[("yT", yi)], w=[("yb_in", g)])
            if j % 2 == 1:
                R.add("pool", lambda e, g=g: e.collective_compute(
                    "AllGather", ALU.bypass, replica_groups=[list(range(NCORES))],
                    ins=[yb_in[g].ap().opt()], outs=[yb_out[g * D:(g + 1) * D, :].opt()]),
                    [("yb_in", g)], ["yb_out"], kind="cc")
                if debug:
                    dma("sp", dbg["y"][g * 256:(g + 1) * 256, :], yb_in[g][:, :], r=[("yb_in", g)], w=["dbg_y"])
        R.barrier()
        A.release(P_TOP)

        zall = A.alloc(16 * TOK * 2, BF16).rearrange("p (k t) -> p k t", k=16)
        M3 = A.mark()
        wmo = A.alloc(16 * D * 2, BF16).rearrange("p (k n) -> p k n", k=16)
        ym = A.alloc(16 * 512 * 2, BF16).rearrange("p (k n) -> p k n", k=16)
        yat = [A.alloc(512 * 4, F32) for _ in range(3)]
        sgt = [A.alloc(512 * 2, BF16) for _ in range(3)]
        gmt = [A.alloc(512 * 2, BF16) for _ in range(2)]
        zpt = [A.alloc(512 * 2, BF16) for _ in range(2)]
        zt = A.alloc(512 * 4, F32)
        for q4 in range(4):
            dma("pool", wmo[:, :, q4 * 512:(q4 + 1) * 512], kmaj(w_mla_o[:, q4 * 512:(q4 + 1) * 512]), w=["wmo"])
        R.add("sp", lambda e: e.dma_start(out=ymine[:, :], in_=yb_out[bass.ts(PID[0], 2 * D), :]),
              ["yb_out"], ["ymine"], kind="d")
        li = 0
        for tt_ in range(4):
            tcol = (tt_ % 2) * 512
            for k in range(16):
                s3 = li % 3
                li += 1
                r0 = (tt_ // 2) * D + k * 128
                dma("sp", yat[s3], ymine[r0:r0 + 128, tcol:tcol + 512], r=["ymine"], w=[("yat", s3)])
                dma("sp", sgt[s3], sg_d[:, k * TOK + tt_ * 512:k * TOK + (tt_ + 1) * 512], r=["sg_d"], w=[("sgt", s3)])
                tt("dve", ym[:, k, :], yat[s3], sgt[s3], ALU.mult, r=[("yat", s3), ("sgt", s3)], w=["ym"])
            for m in range(16):
                s2 = m % 2
                dma("sp", gmt[s2], gm_d[:, m * TOK + tt_ * 512:m * TOK + (tt_ + 1) * 512], r=["gm_d"], w=[("gmt", s2)])
                dma("sp", zpt[s2], zp_d[:, m * TOK + tt_ * 512:m * TOK + (tt_ + 1) * 512], r=["zp_d"], w=[("zpt", s2)])
                b = nbank()
                for k in range(16):
                    mm(ps[b], wmo[:, k, m * 128:(m + 1) * 128], ym[:, k, :], k == 0, k == 15, r=["wmo", "ym"], w=[PS(b)])
                tt("dve", zt, ps[b], gmt[s2], ALU.mult, r=[PS(b), ("gmt", s2)], w=["zt"])
                tt("dve", zall[:, m, tt_ * 512:(tt_ + 1) * 512], zt, zpt[s2], ALU.add, r=["zt", ("zpt", s2)], w=["zall"])
        R.barrier()
        A.release(M3)
        wout = A.alloc(16 * D * 2, BF16).rearrange("p (k n) -> p k n", k=16)
        g1b = A.alloc(D * 4, F32)
        lgb = A.alloc(D * 4, F32)
        lbb = A.alloc(D * 4, F32)
        xt = [A.alloc(D * 4, F32) for _ in range(2)]
        ub = A.alloc(D * 4, F32)
        ob = [A.alloc(D * 4, F32) for _ in range(2)]
        junk = A.alloc(D * 4, F32)
        for q4 in range(4):
            dma("pool", wout[:, :, q4 * 512:(q4 + 1) * 512], kmaj(w_out[:, q4 * 512:(q4 + 1) * 512]), w=["wout"])
        dma("sp", g1b, g1_d[:, :].partition_broadcast(128), r=["g1_d"], w=["g1b"])
        dma("sp", lgb, ln_g[:, :].partition_broadcast(128), w=["lgb"])
        dma("sp", lbb, ln_b[:, :].partition_broadcast(128), w=["lbb"])
        for tb in range(TOK // 128):
            s = tb % 2
            sm = small[:, s * 8:(s + 1) * 8]
            SM = ("sm", s)
            dma("sp", xt[s], xo[HALO + tb * 128:HALO + (tb + 1) * 128, :], w=[("xt", s)])
            hb = 4 * (tb % 2)
            for n4 in range(4):
                for k in range(16):
                    mm(ps[hb + n4], zall[:, k, tb * 128:(tb + 1) * 128], wout[:, k, n4 * 512:(n4 + 1) * 512],
                       k == 0, k == 15, r=["zall", "wout"], w=[PS(hb + n4)])
            for n4 in range(4):
                tt("dve", ub[:, n4 * 512:(n4 + 1) * 512], ps[hb + n4], g1b[:, n4 * 512:(n4 + 1) * 512], ALU.mult,
                   r=[PS(hb + n4), "g1b"], w=["ub"])
            stt("dve", ub, xt[s], ALPHA, ub, ALU.mult, ALU.add, r=[("xt", s), "ub"], w=["ub"])
            memset("dve", sm[:, 0:3], 0.0, [SM])
            act(junk, ub, AF.Identity, r=["ub", SM], w=["junk", SM], accum=sm[:, 0:1])
            ts2("dve", sm[:, 1:2], sm[:, 0:1], -1.0 / D, None, ALU.mult, None, r=[SM], w=[SM])
            act(junk, ub, AF.Square, r=["ub", SM], w=["junk", SM], bias=sm[:, 1:2], accum=sm[:, 2:3])
            act(sm[:, 3:4], sm[:, 2:3], AF.Sqrt, r=[SM, "epsc"], w=[SM], bias=epsc[:, 0:1], scale=1.0 / D)
            R.add("dve", lambda e, o=sm[:, 4:5], i=sm[:, 3:4]: e.reciprocal(o, i), [SM], [SM])
            tt("dve", sm[:, 5:6], sm[:, 1:2], sm[:, 4:5], ALU.mult, r=[SM], w=[SM])
            act(ob[s], ub, AF.Identity, r=["ub", SM], w=[("ob", s)], bias=sm[:, 5:6], scale=sm[:, 4:5])
            tt("pool", ob[s], ob[s], lgb, ALU.mult, r=[("ob", s), "lgb"], w=[("ob", s)])
            tt("pool", ob[s], ob[s], lbb, ALU.add, r=[("ob", s), "lbb"], w=[("ob", s)])
            dma("sp", out_d[tb * 128:(tb + 1) * 128, :], ob[s], r=[("ob", s)], w=["out"])
        R.barrier()

        R.finalize()
        with nc.Block() as block:
            @block.tensor
            def _(e):
                R.emit("pe", e, csem, asems, ccsem)

            @block.scalar
            def _(e):
                R.emit("act", e, csem, asems, ccsem)

            @block.vector
            def _(e):
                R.emit("dve", e, csem, asems, ccsem)

            @block.gpsimd
            def _(e):
                R.emit("pool", e, csem, asems, ccsem)

            @block.sync
            def _(e):
                PID[0] = nc.partition_id([mybir.EngineType.SP])
                R.emit("sp", e, csem, asems, ccsem)
    return nc


def make_in_maps(inputs):
    x = np.asarray(inputs["x"], np.float32)[0]
    pos = np.asarray(inputs["positions"], np.int32)
    g = lambda k: np.ascontiguousarray(np.asarray(inputs[k], np.float32)[0])
    wqb, wkvb = g("w_q_b"), g("w_kv_b")
    invf = (np.float32(10000.0) ** (-np.arange(0, 64, 2, dtype=np.float32) / np.float32(64))).astype(np.float32)
    invf = np.concatenate([invf, invf]).reshape(64, 1)
    ident = np.eye(128, dtype=np.float32)
    tri = (np.arange(128)[:, None] <= np.arange(128)[None, :]).astype(np.float32)
    common = dict(
        c=np.asarray(inputs["c"], np.float32), invf=invf, w_ada=g("w_ada"),
        b_ada=np.asarray(inputs["b_ada"], np.float32), w_in=g("w_in"),
        b_gates=np.asarray(inputs["b_gates"], np.float32), q_norm_g=np.asarray(inputs["q_norm_g"], np.float32),
        kv_norm_g=np.asarray(inputs["kv_norm_g"], np.float32), w_mla_o=g("w_mla_o"), w_pool_g=g("w_pool_g"),
        pool_scale=np.asarray(inputs["pool_scale"], np.float32), w_pool_o=g("w_pool_o"), w_out=g("w_out"),
        ln_g=np.asarray(inputs["ln_g"], np.float32), ln_b=np.asarray(inputs["ln_b"], np.float32),
        ident=ident.astype(ml_dtypes.bfloat16), identf=ident, tri=tri.astype(ml_dtypes.bfloat16),
    )
    maps = []
    for c in range(NCORES):
        t0 = c * TOK
        xo = np.zeros((NT, D), np.float32)
        if c > 0:
            xo[:HALO] = x[t0 - HALO:t0]
        xo[HALO:] = x[t0:t0 + TOK]
        rc = np.zeros((128, 64), np.float32)
        for gi, w in enumerate((2, 4, 8, 16)):
            tglob = t0 + np.arange(16)
            rc[:, gi * 16:(gi + 1) * 16] = (1.0 / np.minimum(tglob + 1, w)).astype(np.float32)[None, :]
        m = dict(common)
        m.update(
            xo=xo, hflag=np.full((128, 1), 0.0 if c == 0 else 1.0, np.float32), rc=rc,
            pos=np.ascontiguousarray(pos[:, t0:t0 + TOK]),
            wq=np.ascontiguousarray(wqb[:, c * 384:(c + 1) * 384]),
            wkv=np.ascontiguousarray(wkvb[:, c * 512:(c + 1) * 512]),
        )
        maps.append(m)
    return maps


_NC_CACHE = {}


def kernel(**inputs):
    if "nc" not in _NC_CACHE:
        _NC_CACHE["nc"] = build()
    nc = _NC_CACHE["nc"]
    maps = make_in_maps(inputs)
    res = run_bass_kernel_spmd(nc, maps, core_ids=list(range(NCORES)))
    out = np.concatenate([np.asarray(res.results[c]["out"], np.float32) for c in range(NCORES)], axis=0)
    return out.reshape(1, S, D)
```
